# Optimizing a Trainium2 kernel written in Bass

```python
import math
import jax
import jax.numpy as jnp
from jax import lax
import numpy as np

D_MODEL = 1024
BATCH = 8
SEQ = 2048
DEPTH = 4

CTX_LEN = 256
GRID_W = 64
HEAD_DIM = 64
MIX_W = D_MODEL
A_HEADS = D_MODEL // (4 * HEAD_DIM)
A_DK = HEAD_DIM // 2
A_DV = HEAD_DIM
A_W = A_HEADS * HEAD_DIM
B_HEADS = 3 * D_MODEL // (8 * HEAD_DIM)
B_KV_HEADS = 2
B_W = B_HEADS * HEAD_DIM
B_KV_W = B_KV_HEADS * HEAD_DIM
C_HEADS = 3 * D_MODEL // (8 * HEAD_DIM)
C_W = C_HEADS * HEAD_DIM
WIN_H = 8
WIN_W = 16
Q_BLOCK = 128
ROPE_THETA = 10000.0
EPS = 1e-6
SPLIT_SIZES = (A_W, A_W, A_W, A_W, B_W, B_KV_W, B_KV_W, B_W, C_W, C_W, C_W, C_W)
IN_W = sum(SPLIT_SIZES)

kernel_name = 'hybrid_diff_gqa_natten_prefix_dit'


def rms_norm(x, g):
    xf = x.astype(jnp.float32)
    y = xf * lax.rsqrt(jnp.mean(xf * xf, axis=-1, keepdims=True) + EPS)
    return (y * g.astype(jnp.float32)).astype(x.dtype)


def axial_rope(x, row, col):
    d = x.shape[-1]
    nf = d // 4
    inv = ROPE_THETA ** (-jnp.arange(nf, dtype=jnp.float32) / nf)
    ang = jnp.concatenate([row[:, None] * inv, col[:, None] * inv], axis=-1)
    shape = (1, x.shape[1]) + (1,) * (x.ndim - 3) + (d // 2,)
    cos = jnp.cos(ang).reshape(shape)
    sin = jnp.sin(ang).reshape(shape)
    xf = x.astype(jnp.float32)
    x1, x2 = xf[..., : d // 2], xf[..., d // 2:]
    return jnp.concatenate([x1 * cos - x2 * sin, x2 * cos + x1 * sin], axis=-1).astype(x.dtype)


def map_query_blocks(fn, q):
    b, s = q.shape[:2]
    nb = s // Q_BLOCK
    qb = jnp.moveaxis(q.reshape((b, nb, Q_BLOCK) + q.shape[2:]), 1, 0)
    out = lax.map(fn, qb)
    return jnp.moveaxis(out, 0, 1).reshape((b, s) + out.shape[3:])


def diff_attend(q, k, v, lam):
    s = jnp.einsum('bqhmd,bkhmd->bhmqk', q, k).astype(jnp.float32) * (q.shape[-1] ** -0.5)
    p = jax.nn.softmax(s, axis=-1)
    a = (p[:, :, 0] - lam * p[:, :, 1]).astype(v.dtype)
    return jnp.einsum('bhqk,bkhd->bqhd', a, v)


def gqa_attend(q, k, v):
    b, nq, h, d = q.shape
    kvh = k.shape[2]
    qg = q.reshape(b, nq, kvh, h // kvh, d)
    s = jnp.einsum('bqhgd,bkhd->bhgqk', qg, k).astype(jnp.float32) * (d ** -0.5)
    p = jax.nn.softmax(s, axis=-1).astype(v.dtype)
    o = jnp.einsum('bhgqk,bkhd->bqhgd', p, v)
    return o.reshape(b, nq, h * v.shape[-1])


def neighbourhood_attend(q, k, v, k_ctx, v_ctx, rpb, rows):
    b, s, h, d = q.shape
    kh = min(WIN_H, rows)
    kw = WIN_W
    scale = d ** -0.5
    qg = q.reshape(b, rows, GRID_W, h, d)
    kg = k.reshape(b, rows, GRID_W, h, d)
    vg = v.reshape(b, rows, GRID_W, h, d)
    r = jnp.arange(rows)
    cidx = jnp.arange(GRID_W)
    r0 = jnp.clip(r - kh // 2, 0, rows - kh)
    key_rows = r0[:, None] + jnp.arange(kh)
    k_blk = kg[:, key_rows]
    v_blk = vg[:, key_rows]
    c0 = jnp.clip(cidx - kw // 2, 0, GRID_W - kw)
    in_win = (cidx[None, :] >= c0[:, None]) & (cidx[None, :] < c0[:, None] + kw)
    dr = key_rows - r[:, None] + (WIN_H - 1)
    dc = jnp.clip(cidx[None, :] - cidx[:, None] + (WIN_W - 1), 0, 2 * WIN_W - 2)
    bias = rpb[:, dr[:, None, :, None], dc[None, :, None, :]]
    s_nb = jnp.einsum('brqhd,brkwhd->bhrqkw', qg, k_blk).astype(jnp.float32) * scale
    s_nb = s_nb + bias.astype(jnp.float32)[None]
    s_nb = jnp.where(in_win[:, None, :], s_nb, -jnp.inf)
    s_cx = jnp.einsum('brqhd,bchd->bhrqc', qg, k_ctx).astype(jnp.float32) * scale
    n_nb = kh * GRID_W
    sc = jnp.concatenate([s_nb.reshape(b, h, rows, GRID_W, n_nb), s_cx], axis=-1)
    p = jax.nn.softmax(sc, axis=-1).astype(v.dtype)
    p_nb = p[..., :n_nb].reshape(b, h, rows, GRID_W, kh, GRID_W)
    p_cx = p[..., n_nb:]
    o = jnp.einsum('bhrqkw,brkwhd->brqhd', p_nb, v_blk) + jnp.einsum('bhrqc,bchd->brqhd', p_cx, v_ctx)
    return o.reshape(b, s, h * d)


def split_proj(p):
    idx = np.cumsum(SPLIT_SIZES)[:-1].tolist()
    return jnp.split(p, idx, axis=-1)


def setup_inputs(seed: int = 0) -> dict:
    key = jax.random.key(seed)
    ks = jax.random.split(key, 22)
    nrm = jax.random.normal
    f32 = jnp.float32
    return {
        'x': nrm(ks[0], (BATCH, SEQ, D_MODEL), f32),
        'c': nrm(ks[1], (BATCH, D_MODEL), f32),
        'ctx': nrm(ks[2], (BATCH, CTX_LEN, D_MODEL), f32),
        'c_ctx': nrm(ks[3], (D_MODEL,), f32),
        'norm_g': 1.0 + 0.05 * nrm(ks[4], (DEPTH, D_MODEL), f32),
        'w_ada': nrm(ks[5], (DEPTH, D_MODEL, 3 * D_MODEL), f32) * (0.5 * D_MODEL ** -0.5),
        'b_ada': 0.01 * nrm(ks[6], (DEPTH, 3 * D_MODEL), f32),
        'w_in': nrm(ks[7], (DEPTH, D_MODEL, IN_W), f32) * (D_MODEL ** -0.5),
        'w_out': nrm(ks[8], (DEPTH, MIX_W, D_MODEL), f32) * (MIX_W ** -0.5),
        'diff_q_norm': 1.0 + 0.05 * nrm(ks[9], (DEPTH, A_DK), f32),
        'diff_k_norm': 1.0 + 0.05 * nrm(ks[10], (DEPTH, A_DK), f32),
        'lambda_q1': 0.1 * nrm(ks[11], (DEPTH, A_DK), f32),
        'lambda_k1': 0.1 * nrm(ks[12], (DEPTH, A_DK), f32),
        'lambda_q2': 0.1 * nrm(ks[13], (DEPTH, A_DK), f32),
        'lambda_k2': 0.1 * nrm(ks[14], (DEPTH, A_DK), f32),
        'diff_subln': 1.0 + 0.05 * nrm(ks[15], (DEPTH, A_DV), f32),
        'gqa_q_norm': 1.0 + 0.05 * nrm(ks[16], (DEPTH, HEAD_DIM), f32),
        'gqa_k_norm': 1.0 + 0.05 * nrm(ks[17], (DEPTH, HEAD_DIM), f32),
        'nat_q_norm': 1.0 + 0.05 * nrm(ks[18], (DEPTH, HEAD_DIM), f32),
        'nat_k_norm': 1.0 + 0.05 * nrm(ks[19], (DEPTH, HEAD_DIM), f32),
        'nat_rpb': 0.1 * nrm(ks[20], (DEPTH, C_HEADS, 2 * WIN_H - 1, 2 * WIN_W - 1), f32),
    }


def reference(x, c, ctx, c_ctx, norm_g, w_ada, b_ada, w_in, w_out, diff_q_norm, diff_k_norm,
              lambda_q1, lambda_k1, lambda_q2, lambda_k2, diff_subln, gqa_q_norm, gqa_k_norm,
              nat_q_norm, nat_k_norm, nat_rpb):
    b, s, _ = x.shape
    n_ctx = ctx.shape[1]
    rows = s // GRID_W
    t = jnp.arange(s)
    row = (t // GRID_W).astype(jnp.float32)
    col = (t % GRID_W).astype(jnp.float32)
    for l in range(DEPTH):
        lam_init = 0.8 - 0.6 * math.exp(-0.3 * l)
        sh, sc, gt = jnp.split(jax.nn.silu(c) @ w_ada[l] + b_ada[l], 3, axis=-1)
        sh_c, sc_c, gt_c = jnp.split(jax.nn.silu(c_ctx) @ w_ada[l] + b_ada[l], 3, axis=-1)
        h = rms_norm(x, norm_g[l]) * (1 + sc[:, None]) + sh[:, None]
        hc = rms_norm(ctx, norm_g[l]) * (1 + sc_c) + sh_c
        qa, ka, va, ga, qb, kb, vb, gb, qn, kn, vn, gn = split_proj(h @ w_in[l])
        qa_c, ka_c, va_c, ga_c, qb_c, kb_c, vb_c, gb_c, qn_c, kn_c, vn_c, gn_c = split_proj(hc @ w_in[l])

        lam = (jnp.exp(jnp.sum(lambda_q1[l] * lambda_k1[l])) - jnp.exp(jnp.sum(lambda_q2[l] * lambda_k2[l]))
               + lam_init).astype(jnp.float32)
        qa = axial_rope(rms_norm(qa.reshape(b, s, A_HEADS, 2, A_DK), diff_q_norm[l]), row, col)
        ka = axial_rope(rms_norm(ka.reshape(b, s, A_HEADS, 2, A_DK), diff_k_norm[l]), row, col)
        ka_c = rms_norm(ka_c.reshape(b, n_ctx, A_HEADS, 2, A_DK), diff_k_norm[l])
        va_c = va_c.reshape(b, n_ctx, A_HEADS, A_DV)
        ka_all = jnp.concatenate([ka, ka_c], axis=1)
        va_all = jnp.concatenate([va.reshape(b, s, A_HEADS, A_DV), va_c], axis=1)
        oa = map_query_blocks(lambda q_blk: diff_attend(q_blk, ka_all, va_all, lam), qa)
        ya = (rms_norm(oa, diff_subln[l]) * (1 - lam_init)).reshape(b, s, A_W) * jax.nn.silu(ga)

        qb = axial_rope(rms_norm(qb.reshape(b, s, B_HEADS, HEAD_DIM), gqa_q_norm[l]), row, col)
        kb = axial_rope(rms_norm(kb.reshape(b, s, B_KV_HEADS, HEAD_DIM), gqa_k_norm[l]), row, col)
        kb_c = rms_norm(kb_c.reshape(b, n_ctx, B_KV_HEADS, HEAD_DIM), gqa_k_norm[l])
        vb_c = vb_c.reshape(b, n_ctx, B_KV_HEADS, HEAD_DIM)
        kb_all = jnp.concatenate([kb, kb_c], axis=1)
        vb_all = jnp.concatenate([vb.reshape(b, s, B_KV_HEADS, HEAD_DIM), vb_c], axis=1)
        yb = map_query_blocks(lambda q_blk: gqa_attend(q_blk, kb_all, vb_all), qb) * jax.nn.silu(gb)

        qn = rms_norm(qn.reshape(b, s, C_HEADS, HEAD_DIM), nat_q_norm[l])
        kn = rms_norm(kn.reshape(b, s, C_HEADS, HEAD_DIM), nat_k_norm[l])
        kn_c = rms_norm(kn_c.reshape(b, n_ctx, C_HEADS, HEAD_DIM), nat_k_norm[l])
        vn_c = vn_c.reshape(b, n_ctx, C_HEADS, HEAD_DIM)
        yn = neighbourhood_attend(qn, kn, vn.reshape(b, s, C_HEADS, HEAD_DIM), kn_c, vn_c, nat_rpb[l], rows)
        yn = yn * jax.nn.silu(gn)

        y = jnp.concatenate([ya, yb, yn], axis=-1) @ w_out[l]

        if l < DEPTH - 1:
            qa_c = rms_norm(qa_c.reshape(b, n_ctx, A_HEADS, 2, A_DK), diff_q_norm[l])
            oa_c = diff_attend(qa_c, ka_c, va_c, lam)
            ya_c = (rms_norm(oa_c, diff_subln[l]) * (1 - lam_init)).reshape(b, n_ctx, A_W) * jax.nn.silu(ga_c)
            qb_c = rms_norm(qb_c.reshape(b, n_ctx, B_HEADS, HEAD_DIM), gqa_q_norm[l])
            yb_c = gqa_attend(qb_c, kb_c, vb_c) * jax.nn.silu(gb_c)
            qn_c = rms_norm(qn_c.reshape(b, n_ctx, C_HEADS, HEAD_DIM), nat_q_norm[l])
            yn_c = gqa_attend(qn_c, kn_c, vn_c) * jax.nn.silu(gn_c)
            ctx = ctx + gt_c * (jnp.concatenate([ya_c, yb_c, yn_c], axis=-1) @ w_out[l])

        x = x + gt[:, None] * y
    return x
```

```python
import math
import itertools
import numpy as np
import ml_dtypes
import concourse.bass as bass
import concourse.mybir as mybir
from concourse.bass_utils import run_bass_kernel_spmd
from contextlib import ExitStack

F32 = mybir.dt.float32
BF16 = mybir.dt.bfloat16
AF = mybir.ActivationFunctionType
ALU = mybir.AluOpType
AX = mybir.AxisListType

D = 1024
SEQ = 2048
NCTX = 256
TOK = SEQ + NCTX
NT = TOK // 128
NLT = SEQ // 128
DEPTH = 4
IN_W = 3584
EPS = 1e-6
NEG = -30000.0
SOFTMAX_SHIFT = -12.0
GRID_W = 64
UINT_M = 22
UFULL_M = 15
UW = (UINT_M + UFULL_M) * 64
UINT_C = 10
UFULL_C = 7

RC = dict(dqn=0, dkn=32, subln=64, gqn=128, gkn=192, nqn=256, nkn=320, lq1=384, lk1=416, lq2=448, lk2=480)
RCW = 512

GROUPS = [
    dict(name='A', kind='A', hd=32, nh=8, nv=4, q=(0, 256), k=(256, 256), v=(512, 256), g=(768, 256),
         wo=(0, 256), rope='A', qg='dqn', kg='dkn'),
    dict(name='B', kind='B', hd=64, nh=6, nv=2, q=(1024, 384), k=(1408, 128), v=(1536, 128), g=(1664, 384),
         wo=(256, 384), rope='B', qg='gqn', kg='gkn'),
    dict(name='C1', kind='C', hd=64, nh=3, nv=3, q=(2048, 192), k=(2432, 192), v=(2816, 192), g=(3200, 192),
         wo=(640, 192), rope=None, qg='nqn', kg='nkn', h0=0),
    dict(name='C2', kind='C', hd=64, nh=3, nv=3, q=(2240, 192), k=(2624, 192), v=(3008, 192), g=(3392, 192),
         wo=(832, 192), rope=None, qg='nqn', kg='nkn', h0=3),
]


class LT:
    __slots__ = ('name', 'w', 'r', 'dsem', 'dtot')

    def __init__(self, name):
        self.name = name
        self.w = None
        self.r = {}
        self.dsem = None
        self.dtot = 0


class Prog:
    def __init__(self, nc, es):
        self.nc = nc
        self.es = es
        self.engs = ('pe', 'act', 'dve', 'pool', 'sp')
        self.sem = {k: es.enter_context(nc.semaphore('s_' + k)) for k in self.engs}
        self.cnt = {k: 0 for k in self.engs}
        self.seen = {k: {} for k in self.engs}
        self.q = {k: [] for k in self.engs}
        self.nops = 0
        self.nwaits = 0

    def sbuf(self, name, shape, dt):
        return self.es.enter_context(self.nc.sbuf_tensor('sb_' + name, shape, dt))

    def psum(self, name, shape, dt):
        return self.es.enter_context(self.nc.psum_tensor('ps_' + name, shape, dt))

    def _wait(self, e, dep):
        key, sem, val = dep
        if key == 'pe' and e == 'pe':
            return
        if self.seen[e].get(key, 0) >= val:
            return
        self.q[e].append(('w', sem, val))
        self.seen[e][key] = val
        self.nwaits += 1

    def deps(self, e, reads, writes):
        for t in reads:
            if t.w is not None:
                self._wait(e, t.w)
        for t in writes:
            if t.w is not None:
                self._wait(e, t.w)
            for d in t.r.values():
                self._wait(e, d)

    def op(self, e, fn, reads=(), writes=()):
        self.deps(e, reads, writes)
        self.cnt[e] += 1
        self.q[e].append(('o', fn, self.sem[e], 1))
        d = (e, self.sem[e], self.cnt[e])
        for t in reads:
            t.r[e] = d
        for t in writes:
            t.w = d
            t.r = {}
        self.nops += 1

    def mm(self, fn, reads=(), writes=(), signal=True):
        e = 'pe'
        self.deps(e, reads, writes)
        if signal:
            self.cnt[e] += 1
            self.q[e].append(('o', fn, self.sem[e], 1))
            d = (e, self.sem[e], self.cnt[e])
        else:
            self.q[e].append(('o', fn, None, 0))
            d = (e, self.sem[e], self.cnt[e] + 1)
        for t in reads:
            t.r[e] = d
        for t in writes:
            t.w = d
            t.r = {}
        self.nops += 1

    def dma_in(self, t, out_ap, in_ap, q='sp'):
        if t.dsem is None:
            t.dsem = self.es.enter_context(self.nc.semaphore('d_' + t.name))
        self.deps(q, (), (t,))
        self.q[q].append(('o', lambda e, o=out_ap, i=in_ap: e.dma_start(out=o, in_=i), t.dsem, 16))
        t.dtot += 16
        t.w = (('dma', t.name), t.dsem, t.dtot)
        t.r = {}
        self.nops += 1

    def dma_out(self, t, out_ap, in_ap, h, q='sp'):
        if h.dsem is None:
            h.dsem = self.es.enter_context(self.nc.semaphore('d_' + h.name))
        self.deps(q, (t,), ())
        self.q[q].append(('o', lambda e, o=out_ap, i=in_ap: e.dma_start(out=o, in_=i), h.dsem, 16))
        h.dtot += 16
        t.r[('dma', h.name)] = (('dma', h.name), h.dsem, h.dtot)
        self.nops += 1

    def dma_out_multi(self, tiles, out_ap, in_ap, h, q='sp'):
        if h.dsem is None:
            h.dsem = self.es.enter_context(self.nc.semaphore('d_' + h.name))
        self.deps(q, tuple(tiles), ())
        self.q[q].append(('o', lambda e, o=out_ap, i=in_ap: e.dma_start(out=o, in_=i), h.dsem, 16))
        h.dtot += 16
        for t in tiles:
            t.r[('dma', h.name)] = (('dma', h.name), h.dsem, h.dtot)
        self.nops += 1

    def wait_dma_done(self, h, q='sp'):
        self.q[q].append(('w', h.dsem, h.dtot))

    def emit(self):
        def replay(lst):
            def f(e):
                for it in lst:
                    if it[0] == 'w':
                        e.wait_ge(it[1], it[2])
                    else:
                        ins = it[1](e)
                        if it[2] is not None:
                            ins.then_inc(it[2], it[3])
            return f
        with self.nc.Block() as block:
            block.sync(replay(self.q['sp']))
            block.tensor(replay(self.q['pe']))
            block.scalar(replay(self.q['act']))
            block.vector(replay(self.q['dve']))
            block.gpsimd(replay(self.q['pool']))


def c_key_blocks(j):
    out = []
    qr0 = 8 * j
    if j == 0:
        kbs = range(0, 6)
    elif j == 3:
        kbs = range(10, 16)
    else:
        kbs = range(4 * j - 2, 4 * j + 6)
    for kb in kbs:
        kr0 = 2 * kb
        e = kr0 - qr0
        if j == 0:
            segs = []
            if kb < 4:
                segs.append((0, 320, 'F', UFULL_C - e))
            else:
                segs.append((0, 320, 'I', 0))
            segs.append((320, 512, 'I', UINT_C - e + 5))
        elif j == 3:
            segs = [(0, 256, 'I', UINT_C - e)]
            if kb >= 12:
                segs.append((256, 512, 'F', UFULL_C - e + 4))
            else:
                segs.append((256, 512, 'I', 0))
        else:
            segs = [(0, 512, 'I', UINT_C - e)]
        out.append((kb, segs))
    return out


def build_bias_tables(rpb):
    Lr = rpb.shape[0]
    kc = np.arange(64)[:, None]
    qc = np.arange(64)[None, :]
    c0 = np.clip(qc - 8, 0, 48)
    colvalid = (kc >= c0) & (kc < c0 + 16)
    dcidx = np.clip(kc - qc + 15, 0, 30)
    U = np.full((Lr, 6, 2, 64, UINT_M + UFULL_M, 64), NEG, np.float32)
    for half in range(2):
        for m in range(UINT_M):
            dr = UINT_C + half - m
            if -4 <= dr <= 3:
                vals = rpb[:, :, dr + 7, :][:, :, dcidx]
                U[:, :, half, :, m, :] = np.where(colvalid[None, None], vals, np.float32(NEG))
        for m in range(UFULL_M):
            dr = UFULL_C + half - m
            if -7 <= dr <= 7:
                vals = rpb[:, :, dr + 7, :][:, :, dcidx]
                U[:, :, half, :, UINT_M + m, :] = np.where(colvalid[None, None], vals, np.float32(NEG))
    return np.ascontiguousarray(U.reshape(Lr, 6, 128, UW))


def head_masks():
    m = np.zeros((128, 8), np.float32)
    for i in range(4):
        m[32 * i:32 * i + 32, i] = 1.0
    m[0:64, 4] = 1.0
    m[64:128, 5] = 1.0
    return m


def rope_tables():
    t = np.arange(SEQ)
    row = (t // GRID_W).astype(np.float32)
    col = (t % GRID_W).astype(np.float32)
    outs = []
    for d in (64, 32):
        nf = d // 4
        inv = (10000.0 ** (-np.arange(nf, dtype=np.float32) / nf)).astype(np.float32)
        ang = np.concatenate([row[:, None] * inv, col[:, None] * inv], axis=-1).astype(np.float32)
        outs.append(np.cos(ang).astype(np.float32))
        outs.append(np.sin(ang).astype(np.float32))
    tab = np.concatenate(outs, axis=-1)
    return np.ascontiguousarray(tab.reshape(NLT, 128, 96).transpose(1, 0, 2))


class Builder:
    def __init__(self, layers, last_layer_flags, emit_ctx, dbg=()):
        self.layers = layers
        self.last_flags = last_layer_flags
        self.emit_ctx = emit_ctx
        self.dbg = set(dbg)
        self.nc = bass.Bass("TRN2", target_bir_lowering=False)

    def dram_in(self, name, shape):
        return self.nc.dram_tensor(name, list(shape), F32, kind="ExternalInput").ap()

    def dram_out(self, name, shape, dt=F32):
        return self.nc.dram_tensor(name, list(shape), dt, kind="ExternalOutput").ap()

    def build(self):
        nc = self.nc
        with ExitStack() as es:
            self.P = P = Prog(nc, es)
            nl = len(self.layers)
            self.x_d = self.dram_in("x", [SEQ, D])
            self.ctx_d = self.dram_in("ctx", [NCTX, D])
            self.cmat_d = self.dram_in("cmat", [128, 16])
            self.w_ada_d = self.dram_in("w_ada", [nl, D, 3 * D])
            self.w_in_d = self.dram_in("w_in", [nl, D, IN_W])
            self.w_out_d = self.dram_in("w_out", [nl, D, D])
            self.colc_d = self.dram_in("colc", [128, nl * 32])
            self.rowc_d = self.dram_in("rowc", [128, nl * RCW])
            self.rope_d = self.dram_in("rope", [128, NLT * 96])
            self.ident_d = self.dram_in("ident", [128, 128])
            self.hmask_d = self.dram_in("hmask", [128, 8])
            self.bias_d = self.dram_in("biasu", [nl, 6, 128, UW])
            self.y_d = self.dram_out("y", [SEQ, D])
            if self.emit_ctx:
                self.ctxo_d = self.dram_out("ctx_out", [NCTX, D])
            self.dbg_d = {}
            for name, shape in (('hT', [128, 8 * TOK]), ('kt', [128, 2 * TOK]), ('v', [128, NT * 260]),
                                ('qt', [128, 3 * 512]), ('otok', [128, 4 * 384]), ('ytok', [128, 4 * 384]),
                                ('g', [128, 4 * 384]), ('mod', [128, 48])):
                if name in self.dbg:
                    self.dbg_d[name] = self.dram_out("dbg_" + name, shape, F32 if name in ('otok', 'mod') else BF16)
            self.alloc()
            self.prologue()
            for li, l in enumerate(self.layers):
                self.layer(li, self.last_flags[li])
            self.epilogue()
            P.emit()
        return nc

    def alloc(self):
        P = self.P
        self.x_sb = P.sbuf("x_sb", [128, NT, D], F32)
        self.X = [LT("x%d" % i) for i in range(NT)]
        self.hT = P.sbuf("hT", [128, 8, TOK], BF16)
        self.HT = [LT("hT%d" % i) for i in range(NT)]
        self.kt = P.sbuf("kt", [128, 2, TOK], BF16)
        self.KT = [LT("kt%d" % i) for i in range(NT)]
        self.v = P.sbuf("v", [128, NT, 260], BF16)
        self.V = [LT("v%d" % i) for i in range(NT)]
        self.ws = [P.sbuf("ws%d" % i, [128, 8, 128], F32) for i in range(2)]
        self.WS = [LT("ws%d" % i) for i in range(2)]
        self.wu = [P.sbuf("wu%d" % i, [128, 8, 256], BF16) for i in range(2)]
        self.WU = [LT("wu%d" % i) for i in range(2)]
        self.ws_i = 0
        self.wu_i = 0
        self.wo = P.sbuf("wo", [128, 3, D], BF16)
        self.WO = LT("wo")
        self.qf = P.sbuf("qf", [128, 4, 256], F32); self.QF = LT("qf")
        self.sq = P.sbuf("sq", [128, 4, 256], F32); self.SQ = LT("sq")
        self.qb = P.sbuf("qb", [128, 4, 384], BF16); self.QB = LT("qb")
        self.qm = [P.sbuf("qm%d" % i, [128, 512], BF16) for i in range(3)]; self.QM = [LT("qm%d" % i) for i in range(3)]
        self.qm_i = 0
        self.hmask = P.sbuf("hmask", [128, 8], F32); self.HMASK = LT("hmask")
        self.small2 = P.sbuf("small2", [128, 16], F32); self.SMALL2 = LT("small2")
        self.qTs = [P.sbuf("qT%d" % i, [128, 3, 512], BF16) for i in range(2)]; self.QTs = [LT("qT%d" % i) for i in range(2)]
        self.qT, self.QT = self.qTs[0], self.QTs[0]
        self.g_sb = P.sbuf("g_sb", [128, 4, 384], BF16); self.G = LT("g")
        self.otok = P.sbuf("otok", [128, 4, 384], F32); self.OTOK = LT("otok")
        self.sa = self.otok[:].rearrange("p a b -> p (a b)")[:, 0:1024]; self.SA = self.OTOK
        self.ytok, self.YTOK = self.qb, self.QB
        self.pt = [P.sbuf("pt%d" % i, [128, 512], BF16) for i in range(3)]
        self.PT = [LT("pt%d" % i) for i in range(3)]
        self.pt_i = 0
        self.o01 = [P.sbuf("o01_%d" % i, [128, 4, 64], F32) for i in range(2)]
        self.O01 = [LT("o01_%d" % i) for i in range(2)]
        self.ubi = P.sbuf("ubi", [128, 3, UINT_M * 64], BF16); self.UBI = [LT("ubi%d" % i) for i in range(3)]
        self.ubfus = [P.sbuf("ubfu%d" % i, [128, UFULL_M * 64], BF16) for i in range(2)]
        self.UBFUS = [LT("ubfu%d" % i) for i in range(2)]
        self.small = P.sbuf("small", [128, 320], F32); self.SMALL = LT("small")
        self.rope = P.sbuf("rope", [128, NLT, 96], F32); self.ROPE = LT("rope")
        self.ident = P.sbuf("ident", [128, 128], F32); self.IDENT = LT("ident")
        self.identb = P.sbuf("identb", [128, 128], BF16); self.IDENTB = LT("identb")
        self.onesb = P.sbuf("onesb", [128, 128], BF16); self.ONESB = LT("onesb")
        self.dhl = P.sbuf("dhl", [128, 2, 128], BF16); self.DHL = LT("dhl")
        self.ghl = P.sbuf("ghl", [128, 2, 2, 8], BF16); self.GHL = LT("ghl")
        self.gtmp = P.sbuf("gtmp", [128, 16], F32); self.GTMP = LT("gtmp")
        nl = len(self.layers)
        self.colc = P.sbuf("colc", [128, nl, 32], F32); self.COLC = LT("colc")
        self.rowc = P.sbuf("rowc", [128, RCW], F32); self.ROWC = LT("rowc")
        self.cm = P.sbuf("cm", [128, 8, 2], F32); self.CM = LT("cm")
        self.modc = P.sbuf("modc", [128, 24, 2], F32); self.MODC = LT("modc")
        self.acol = P.sbuf("acol", [128, 8, 2], F32); self.ACOL = LT("acol")
        self.lyr = P.sbuf("lyr", [128, 16], F32); self.LYR = LT("lyr")
        self.gains = P.sbuf("gains", [128, 64], F32); self.GAINS = LT("gains")
        self.gqc = P.sbuf("gqc", [128, 64], F32); self.GQC = LT("gqc")
        self.s_ps = [P.psum("s_ps%d" % i, [128, 512], F32) for i in range(3)]
        self.S = [LT("s_ps%d" % i) for i in range(3)]
        self.s_i = 0
        self.o_ps = [P.psum("o_ps%d" % i, [128, 512], F32) for i in range(2)]
        self.O = [LT("o_ps%d" % i) for i in range(2)]
        self.o_i = 0
        self.m_ps = [P.psum("m_ps%d" % i, [128, 512], F32) for i in range(2)]
        self.M = [LT("m_ps%d" % i) for i in range(2)]
        self.m_i = 0
        self.t_ps = P.psum("t_ps", [128, 1024], BF16)
        self.TPS = LT("t_ps")
        self.OUT = LT("out")
        self.pairs = [[(self.m_ps[0], self.M[0]), (self.m_ps[1], self.M[1])], [(self.s_ps[0], self.S[0]), (self.s_ps[1], self.S[1])]]
        self.pair_i = 0
        self.inter_gen = None

    def prologue(self):
        P = self.P
        for tt in range(NT):
            src = self.x_d[tt * 128:(tt + 1) * 128, :] if tt < NLT else self.ctx_d[(tt - NLT) * 128:(tt - NLT + 1) * 128, :]
            P.dma_in(self.X[tt], self.x_sb[:, tt, :], src)
        P.dma_in(self.IDENT, self.ident[:], self.ident_d)
        P.dma_in(self.HMASK, self.hmask[:], self.hmask_d)
        for i in range(2):
            P.op('pool', lambda e, i=i: e.memset(self.qTs[i][:], 0.0), writes=(self.QTs[i],))
        P.op('pool', lambda e: e.memset(self.kt[:], 0.0), writes=tuple(self.KT))
        P.dma_in(self.ROPE, self.rope[:], self.rope_d.rearrange("p (t c) -> p t c", c=96))
        P.dma_in(self.COLC, self.colc[:], self.colc_d.rearrange("p (l c) -> p l c", c=32))
        P.dma_in(self.CM, self.cm[:], self.cmat_d.rearrange("p (k c) -> p k c", c=2))
        P.op('dve', lambda e: e.tensor_copy(out=self.identb[:], in_=self.ident[:]), reads=(self.IDENT,), writes=(self.IDENTB,))
        P.op('dve', lambda e: e.memset(self.onesb[:], 1.0), writes=(self.ONESB,))
        vv = self.v[:].rearrange("p t (h c) -> p t h c", c=65)
        P.op('dve', lambda e: e.memset(vv[:, :, :, 64:65], 1.0), writes=tuple(self.V))
        P.op('act', lambda e: e.activation(out=self.small[:, 0:16], in_=self.cm[:].rearrange("p k c -> p (k c)"), func=AF.Tanh, scale=0.5),
             reads=(self.CM,), writes=(self.SMALL,))
        P.op('dve', lambda e: e.tensor_scalar(out=self.small[:, 0:16], in0=self.small[:, 0:16], scalar1=0.5, scalar2=0.5, op0=ALU.mult, op1=ALU.add),
             reads=(self.SMALL,), writes=(self.SMALL,))
        P.op('dve', lambda e: e.tensor_tensor(out=self.cm[:].rearrange("p k c -> p (k c)"), in0=self.cm[:].rearrange("p k c -> p (k c)"), in1=self.small[:, 0:16], op=ALU.mult),
             reads=(self.SMALL, self.CM), writes=(self.CM,))

    def next_m(self):
        i = self.m_i
        self.m_i = (i + 1) % 2
        return self.m_ps[i], self.M[i]

    def load_wunit(self, src_ap, ncols, scale_col=None):
        P = self.P
        ui = self.wu_i; self.wu_i = (ui + 1) % 2
        wu, WU = self.wu[ui], self.WU[ui]
        c = 0
        while c < ncols:
            w = min(128, ncols - c)
            si = self.ws_i; self.ws_i = (si + 1) % 2
            ws, WS = self.ws[si], self.WS[si]
            P.dma_in(WS, ws[:, :, 0:w], src_ap[:, c:c + w].rearrange("(k p) n -> p k n", p=128))
            P.op('pool', lambda e, ws=ws, c=c, w=w: e.tensor_copy(out=wu[:, :, c:c + w], in_=ws[:, :, 0:w]), reads=(WS,), writes=(WU,))
            c += w
        return wu, WU

    def adaln(self, li):
        P = self.P
        mp, MP = self.m_ps[0], self.M[0]
        for j in range(24):
            si = self.ws_i; self.ws_i = (si + 1) % 2
            ws, WS = self.ws[si], self.WS[si]
            P.dma_in(WS, ws[:], self.w_ada_d[li][:, j * 128:(j + 1) * 128].rearrange("(k p) n -> p k n", p=128))
            for k in range(8):
                P.mm(lambda e, k=k, j=j, ws=ws: e.matmul(mp[:, 2 * j:2 * j + 2], lhsT=ws[:, k, :], rhs=self.cm[:, k, :], start=(k == 0), stop=(k == 7)),
                     reads=(WS, self.CM), writes=(MP,), signal=(k == 7))
        colc = self.colc
        P.op('dve', lambda e: e.tensor_tensor(out=self.modc[:], in0=mp[:, 0:48].rearrange("p (j c) -> p j c", c=2),
                                              in1=colc[:, li, 8:32].unsqueeze(2).to_broadcast([128, 24, 2]), op=ALU.add),
             reads=(MP, self.COLC), writes=(self.MODC,))
        P.op('dve', lambda e: e.tensor_scalar(out=self.acol[:], in0=self.modc[:, 8:16, :], scalar1=1.0, scalar2=None, op0=ALU.add),
             reads=(self.MODC,), writes=(self.ACOL,))
        P.op('dve', lambda e: e.tensor_tensor(out=self.acol[:], in0=self.acol[:], in1=colc[:, li, 0:8].unsqueeze(2).to_broadcast([128, 8, 2]), op=ALU.mult),
             reads=(self.ACOL, self.COLC), writes=(self.ACOL,))
        for vv in range(2):
            gcol = self.modc[:, 16:24, vv]
            P.op('dve', lambda e, vv=vv, gcol=gcol: e.tensor_copy(out=self.ghl[:, vv, 0, :], in_=gcol), reads=(self.MODC,), writes=(self.GHL,))
            P.op('dve', lambda e, vv=vv, gcol=gcol: e.tensor_tensor(out=self.gtmp[:, vv * 8:(vv + 1) * 8], in0=gcol, in1=self.ghl[:, vv, 0, :], op=ALU.subtract),
                 reads=(self.MODC, self.GHL), writes=(self.GTMP,))
            P.op('dve', lambda e, vv=vv: e.tensor_copy(out=self.ghl[:, vv, 1, :], in_=self.gtmp[:, vv * 8:(vv + 1) * 8]), reads=(self.GTMP, self.GHL), writes=(self.GHL,))
        if 'mod' in self.dbg and li == 0:
            P.dma_out(self.MODC, self.dbg_d['mod'], self.modc[:].rearrange("p j c -> p (j c)"), self.OUT)

    def layer_consts(self, li, l_global):
        P = self.P
        rc = self.rowc
        lam_init = 0.8 - 0.6 * math.exp(-0.3 * l_global)
        sm = self.small
        P.op('dve', lambda e: e.tensor_tensor(out=sm[:, 0:32], in0=rc[:, RC['lq1']:RC['lq1'] + 32], in1=rc[:, RC['lk1']:RC['lk1'] + 32], op=ALU.mult),
             reads=(self.ROWC,), writes=(self.SMALL,))
        P.op('dve', lambda e: e.tensor_tensor(out=sm[:, 32:64], in0=rc[:, RC['lq2']:RC['lq2'] + 32], in1=rc[:, RC['lk2']:RC['lk2'] + 32], op=ALU.mult),
             reads=(self.ROWC, self.SMALL), writes=(self.SMALL,))
        P.op('dve', lambda e: e.tensor_reduce(out=sm[:, 64:66], in_=sm[:, 0:64].rearrange("p (a b) -> p a b", b=32), op=ALU.add, axis=AX.X),
             reads=(self.SMALL,), writes=(self.SMALL,))
        P.op('act', lambda e: e.activation(out=sm[:, 66:68], in_=sm[:, 64:66], func=AF.Exp), reads=(self.SMALL,), writes=(self.SMALL,))
        P.op('dve', lambda e: e.scalar_tensor_tensor(out=self.lyr[:, 0:1], in0=sm[:, 67:68], scalar=-lam_init, op0=ALU.add, in1=sm[:, 66:67], op1=ALU.subtract),
             reads=(self.SMALL,), writes=(self.LYR,))
        for gi, (qn, kn, hd) in enumerate((('dqn', 'dkn', 32), ('gqn', 'gkn', 64), ('nqn', 'nkn', 64))):
            P.op('dve', lambda e, qn=qn, hd=hd: e.tensor_reduce(out=sm[:, 70:71], in_=rc[:, RC[qn]:RC[qn] + hd], op=ALU.max, axis=AX.X, apply_absolute_value=True),
                 reads=(self.ROWC, self.SMALL), writes=(self.SMALL,))
            P.op('dve', lambda e, kn=kn, hd=hd: e.tensor_reduce(out=sm[:, 71:72], in_=rc[:, RC[kn]:RC[kn] + hd], op=ALU.max, axis=AX.X, apply_absolute_value=True),
                 reads=(self.ROWC, self.SMALL), writes=(self.SMALL,))
            P.op('dve', lambda e, gi=gi, hd=hd: e.scalar_tensor_tensor(out=self.lyr[:, 1 + gi:2 + gi], in0=sm[:, 70:71], scalar=-math.sqrt(hd), op0=ALU.mult, in1=sm[:, 71:72], op1=ALU.mult),
                 reads=(self.SMALL, self.LYR), writes=(self.LYR,))
        P.op('dve', lambda e: e.tensor_scalar(out=self.gains[:], in0=rc[:, RC['subln']:RC['subln'] + 64], scalar1=(1.0 - lam_init), scalar2=None, op0=ALU.mult),
             reads=(self.ROWC,), writes=(self.GAINS,))
        P.op('dve', lambda e: e.tensor_scalar(out=self.gqc[:], in0=rc[:, RC['nqn']:RC['nqn'] + 64], scalar1=0.125, scalar2=None, op0=ALU.mult),
             reads=(self.ROWC,), writes=(self.GQC,))

    def build_hT(self, li):
        P = self.P
        sm = self.small
        for tt in range(NT):
            P.op('act', lambda e, tt=tt: e.activation(out=self.sa[:], in_=self.x_sb[:, tt, :], func=AF.Square, accum_out=sm[:, 100 + tt:101 + tt]),
                 reads=(self.X[tt], self.SMALL), writes=(self.SA, self.SMALL))
        P.op('act', lambda e: e.activation(out=sm[:, 120:120 + NT], in_=sm[:, 100:100 + NT], func=AF.Ln, scale=1.0 / D, bias=EPS),
             reads=(self.SMALL,), writes=(self.SMALL,))
        P.op('act', lambda e: e.activation(out=sm[:, 140:140 + NT], in_=sm[:, 120:120 + NT], func=AF.Exp, scale=-0.5),
             reads=(self.SMALL,), writes=(self.SMALL,))
        xh = [(self.sa, self.SA), (self.qf[:].rearrange("p a b -> p (a b)"), self.QF), (self.sq[:].rearrange("p a b -> p (a b)"), self.SQ)]
        for tiles in ([0, 1, 2], [3, 4, 5], [6, 7, 8], [9, 10, 11], [12, 13, 14], [15], [16, 17]):
            t0 = tiles[0]
            nt = len(tiles)
            vv = 0 if t0 < NLT else 1
            for i, tt in enumerate(tiles):
                xs, XS = xh[i]
                P.op('dve', lambda e, tt=tt, xs=xs: e.tensor_scalar(out=xs[:, 0:1024], in0=self.x_sb[:, tt, :], scalar1=sm[:, 140 + tt:141 + tt], scalar2=None, op0=ALU.mult),
                     reads=(self.X[tt], self.SMALL), writes=(XS,))
            for k in range(8):
                mp, MP = self.next_m()
                for i, tt in enumerate(tiles):
                    xs, XS = xh[i]
                    P.mm(lambda e, k=k, i=i, mp=mp, xs=xs: e.transpose(out=mp[:, i * 128:(i + 1) * 128], in_=xs[:, k * 128:(k + 1) * 128], identity=self.ident[:]),
                         reads=(XS, self.IDENT), writes=(MP,), signal=(i == nt - 1))
                P.op('act', lambda e, k=k, mp=mp, t0=t0, vv=vv, nt=nt: e.activation(out=self.hT[:, k, t0 * 128:(t0 + nt) * 128], in_=mp[:, 0:nt * 128], func=AF.Identity,
                                                                           scale=self.acol[:, k, vv:vv + 1], bias=self.modc[:, k, vv:vv + 1]),
                     reads=(MP, self.ACOL, self.MODC), writes=tuple(self.HT[tt] for tt in tiles))
        if 'hT' in self.dbg and li == 0:
            P.dma_out_multi(self.HT, self.dbg_d['hT'], self.hT[:].rearrange("p k t -> p (k t)"), self.OUT)

    def project(self, wu, WU, ncols, tt):
        P = self.P
        mp, MP = self.next_m()
        for k in range(8):
            P.mm(lambda e, k=k: e.matmul(mp[:, 0:ncols], lhsT=self.hT[:, k, tt * 128:(tt + 1) * 128], rhs=wu[:, k, 0:ncols], start=(k == 0), stop=(k == 7)),
                 reads=(self.HT[tt], WU), writes=(MP,), signal=(k == 7))
        return mp, MP

    def norm_rope(self, src, SRC, ncols, hd, gain_ap, rope_kind, tt, out_ap, OUTT, GAINT=None):
        P = self.P
        nh = ncols // hd
        half = hd // 2
        sm = self.small
        qf3 = self.qf[:, 0:ncols].rearrange("p (h d) -> p h d", d=hd)
        P.op('act', lambda e: e.activation(out=self.qf[:, 0:ncols], in_=src, func=AF.Copy), reads=(SRC,), writes=(self.QF,))
        P.op('act', lambda e: e.activation(out=self.sq[:, 0:ncols], in_=src, func=AF.Square), reads=(SRC,), writes=(self.SQ,))
        P.op('dve', lambda e: e.tensor_reduce(out=sm[:, 200:200 + nh], in_=self.sq[:, 0:ncols].rearrange("p (h d) -> p h d", d=hd), op=ALU.add, axis=AX.X),
             reads=(self.SQ, self.SMALL), writes=(self.SMALL,))
        P.op('act', lambda e: e.activation(out=sm[:, 220:220 + nh], in_=sm[:, 200:200 + nh], func=AF.Ln, scale=1.0 / hd, bias=EPS),
             reads=(self.SMALL,), writes=(self.SMALL,))
        P.op('act', lambda e: e.activation(out=sm[:, 240:240 + nh], in_=sm[:, 220:220 + nh], func=AF.Exp, scale=-0.5),
             reads=(self.SMALL,), writes=(self.SMALL,))
        P.op('dve', lambda e: e.tensor_tensor(out=qf3, in0=qf3, in1=sm[:, 240:240 + nh].unsqueeze(2).to_broadcast([128, nh, hd]), op=ALU.mult),
             reads=(self.QF, self.SMALL), writes=(self.QF,))
        gain_b = gain_ap.unsqueeze(1).to_broadcast([128, nh, hd])
        GAINT = GAINT or self.ROWC
        if rope_kind is None or tt >= NLT:
            P.op('dve', lambda e: e.tensor_tensor(out=out_ap.rearrange("p (h d) -> p h d", d=hd), in0=qf3, in1=gain_b, op=ALU.mult),
                 reads=(self.QF, GAINT), writes=(OUTT,))
            return
        P.op('dve', lambda e: e.tensor_tensor(out=qf3, in0=qf3, in1=gain_b, op=ALU.mult), reads=(self.QF, GAINT), writes=(self.QF,))
        c0 = 0 if rope_kind == 'B' else 64
        cos = self.rope[:, tt, c0:c0 + half]
        sin = self.rope[:, tt, c0 + half:c0 + 2 * half]
        q4 = self.qf[:, 0:ncols].rearrange("p (h two f) -> p h two f", two=2, f=half)
        tmp = self.sq[:, 0:ncols].rearrange("p (h two f) -> p h two f", two=2, f=half)
        o4 = out_ap.rearrange("p (h two f) -> p h two f", two=2, f=half)
        cosb = cos.unsqueeze(1).to_broadcast([128, nh, half])
        sinb = sin.unsqueeze(1).to_broadcast([128, nh, half])
        P.op('pool', lambda e: e.tensor_tensor(out=tmp[:, :, 0, :], in0=q4[:, :, 1, :], in1=sinb, op=ALU.mult), reads=(self.QF, self.ROPE), writes=(self.SQ,))
        P.op('dve', lambda e: e.tensor_tensor(out=tmp[:, :, 1, :], in0=q4[:, :, 0, :], in1=cosb, op=ALU.mult), reads=(self.QF, self.ROPE, self.SQ), writes=(self.SQ,))
        P.op('dve', lambda e: e.tensor_tensor(out=o4[:, :, 0, :], in0=tmp[:, :, 1, :], in1=tmp[:, :, 0, :], op=ALU.subtract), reads=(self.SQ,), writes=(OUTT,))
        P.op('pool', lambda e: e.tensor_tensor(out=tmp[:, :, 0, :], in0=q4[:, :, 0, :], in1=sinb, op=ALU.mult), reads=(self.QF, self.ROPE, self.SQ), writes=(self.SQ,))
        P.op('dve', lambda e: e.tensor_tensor(out=tmp[:, :, 1, :], in0=q4[:, :, 1, :], in1=cosb, op=ALU.mult), reads=(self.QF, self.ROPE, self.SQ), writes=(self.SQ,))
        P.op('dve', lambda e: e.tensor_tensor(out=o4[:, :, 1, :], in0=tmp[:, :, 1, :], in1=tmp[:, :, 0, :], op=ALU.add), reads=(self.SQ, OUTT), writes=(OUTT,))

    def transpose_to(self, src, SRC, ncols, dst_fn, DST):
        P = self.P
        nch = (ncols + 127) // 128
        for c in range(nch):
            w = min(128, ncols - c * 128)
            P.mm(lambda e, c=c, w=w: e.transpose(out=self.t_ps[0:w, c * 128:(c + 1) * 128], in_=src[:, c * 128:c * 128 + w], identity=self.identb[:]),
                 reads=(SRC, self.IDENTB), writes=(self.TPS,), signal=(c == nch - 1))
        for c in range(nch):
            w = min(128, ncols - c * 128)
            P.op('dve', lambda e, c=c, w=w: e.tensor_copy(out=dst_fn(c, w), in_=self.t_ps[0:w, c * 128:(c + 1) * 128]), reads=(self.TPS,), writes=(DST,))

    def proj_batch(self, wu, WU, n, tiles, force_pair=None):
        P = self.P
        if force_pair is None:
            pair = self.pairs[self.pair_i]; self.pair_i ^= 1
        else:
            pair = self.pairs[force_pair]
        for i, tt in enumerate(tiles):
            ps, PS = pair[i // 2]
            off = (i % 2) * 256
            for k in range(8):
                P.mm(lambda e, k=k, ps=ps, off=off, tt=tt: e.matmul(ps[:, off:off + n], lhsT=self.hT[:, k, tt * 128:(tt + 1) * 128], rhs=wu[:, k, 0:n], start=(k == 0), stop=(k == 7)),
                     reads=(self.HT[tt], WU), writes=(PS,), signal=(k == 7))
        return pair

    def bank_views(self, pair, nt, n):
        out = []
        for b in range((nt + 1) // 2):
            ntb = min(2, nt - 2 * b)
            ps, PS = pair[b]
            out.append((b, ntb, ps[:, 0:512].rearrange("p (t c) -> p t c", c=256)[:, 0:ntb, 0:n], PS))
        return out

    def norm_rope_batch(self, pair, n, hd, gain_ap, GAINT, rope_kind, tiles):
        P = self.P
        nt = len(tiles)
        nh = n // hd
        half = hd // 2
        sm = self.small
        for (b, ntb, src, PS) in self.bank_views(pair, nt, n):
            P.op('act', lambda e, b=b, ntb=ntb, src=src: e.activation(out=self.sq[:, 2 * b:2 * b + ntb, 0:n], in_=src, func=AF.Square), reads=(PS,), writes=(self.SQ,))
            gb = gain_ap.unsqueeze(1).to_broadcast([128, nh, hd])
            for t in range(ntb):
                P.op('dve', lambda e, b=b, t=t, src=src, gb=gb: e.tensor_tensor(out=self.qf[:, 2 * b + t, 0:n].rearrange("p (h d) -> p h d", d=hd),
                                                                             in0=src[:, t, :].rearrange("p (h d) -> p h d", d=hd), in1=gb, op=ALU.mult),
                     reads=(PS, GAINT, self.SQ), writes=(self.QF,))
        m = nt * nh
        q4 = self.qf[:, 0:nt, 0:n].rearrange("p t (h d) -> p t h d", d=hd)
        ss = sm[:, 200:200 + m].rearrange("p (t h) -> p t h", h=nh)
        P.op('dve', lambda e: e.tensor_reduce(out=ss, in_=self.sq[:, 0:nt, 0:n].rearrange("p t (h d) -> p t h d", d=hd), op=ALU.add, axis=AX.X),
             reads=(self.SQ, self.SMALL), writes=(self.SMALL,))
        P.op('act', lambda e: e.activation(out=sm[:, 240:240 + m], in_=sm[:, 200:200 + m], func=AF.Ln, scale=1.0 / hd, bias=EPS), reads=(self.SMALL,), writes=(self.SMALL,))
        P.op('act', lambda e: e.activation(out=sm[:, 280:280 + m], in_=sm[:, 240:240 + m], func=AF.Exp, scale=-0.5), reads=(self.SMALL,), writes=(self.SMALL,))
        rstd_b = sm[:, 280:280 + m].rearrange("p (t h) -> p t h", h=nh).unsqueeze(3).to_broadcast([128, nt, nh, hd])
        gain_b = gain_ap.unsqueeze(1).unsqueeze(1).to_broadcast([128, nt, nh, hd])
        o4 = self.qb[:, 0:nt, 0:n].rearrange("p t (h d) -> p t h d", d=hd)
        if rope_kind is None or tiles[0] >= NLT:
            P.op('dve', lambda e: e.tensor_tensor(out=o4, in0=q4, in1=rstd_b, op=ALU.mult), reads=(self.QF, self.SMALL), writes=(self.QB,))
            return
        P.op('dve', lambda e: e.tensor_tensor(out=q4, in0=q4, in1=rstd_b, op=ALU.mult), reads=(self.QF, self.SMALL), writes=(self.QF,))
        c0 = 0 if rope_kind == 'B' else 64
        tt0 = tiles[0]
        cosb = self.rope[:, tt0:tt0 + nt, c0:c0 + half].unsqueeze(2).to_broadcast([128, nt, nh, half])
        sinb = self.rope[:, tt0:tt0 + nt, c0 + half:c0 + 2 * half].unsqueeze(2).to_broadcast([128, nt, nh, half])
        q5 = self.qf[:, 0:nt, 0:n].rearrange("p t (h two f) -> p t h two f", two=2, f=half)
        t5 = self.sq[:, 0:nt, 0:n].rearrange("p t (h two f) -> p t h two f", two=2, f=half)
        o5 = self.qb[:, 0:nt, 0:n].rearrange("p t (h two f) -> p t h two f", two=2, f=half)
        x1, x2 = q5[:, :, :, 0, :], q5[:, :, :, 1, :]
        ta, tb = t5[:, :, :, 0, :], t5[:, :, :, 1, :]
        P.op('dve', lambda e: e.tensor_tensor(out=ta, in0=x2, in1=sinb, op=ALU.mult), reads=(self.QF, self.ROPE), writes=(self.SQ,))
        P.op('dve', lambda e: e.tensor_tensor(out=tb, in0=x1, in1=cosb, op=ALU.mult), reads=(self.QF, self.ROPE, self.SQ), writes=(self.SQ,))
        P.op('dve', lambda e: e.tensor_tensor(out=o5[:, :, :, 0, :], in0=tb, in1=ta, op=ALU.subtract), reads=(self.SQ,), writes=(self.QB,))
        P.op('dve', lambda e: e.tensor_tensor(out=ta, in0=x1, in1=sinb, op=ALU.mult), reads=(self.QF, self.ROPE, self.SQ), writes=(self.SQ,))
        P.op('dve', lambda e: e.tensor_tensor(out=tb, in0=x2, in1=cosb, op=ALU.mult), reads=(self.QF, self.ROPE, self.SQ), writes=(self.SQ,))
        P.op('dve', lambda e: e.tensor_tensor(out=o5[:, :, :, 1, :], in0=tb, in1=ta, op=ALU.add), reads=(self.SQ, self.QB), writes=(self.QB,))

    def transpose_batch(self, n, nt, dst_fn, DSTS, extra_swap=None):
        P = self.P
        nch = (n + 127) // 128
        last = (nch - 1, nt - 1)
        for c in range(nch):
            w = min(128, n - c * 128)
            for i in range(nt):
                blk = c * nt + i
                P.mm(lambda e, c=c, w=w, i=i, blk=blk: e.transpose(out=self.t_ps[0:w, blk * 128:(blk + 1) * 128], in_=self.qb[:, i, c * 128:c * 128 + w], identity=self.identb[:]),
                     reads=(self.QB, self.IDENTB), writes=(self.TPS,), signal=((c, i) == last and extra_swap is None))
        if extra_swap is not None:
            for i in range(nt):
                blk = nt + i
                P.mm(lambda e, i=i, blk=blk: e.transpose(out=self.t_ps[0:64, blk * 128:(blk + 1) * 128], in_=self.qb[:, i, 64:128], identity=self.identb[:]),
                     reads=(self.QB, self.IDENTB), writes=(self.TPS,), signal=False)
                P.mm(lambda e, i=i, blk=blk: e.transpose(out=self.t_ps[64:128, blk * 128:(blk + 1) * 128], in_=self.qb[:, i, 0:64], identity=self.identb[:]),
                     reads=(self.QB, self.IDENTB), writes=(self.TPS,), signal=(i == nt - 1))
        for c in range(nch):
            w = min(128, n - c * 128)
            dst = dst_fn(c, w)
            P.op('dve', lambda e, c=c, w=w, dst=dst: e.tensor_copy(out=dst, in_=self.t_ps[0:w, c * nt * 128:(c + 1) * nt * 128]), reads=(self.TPS,), writes=tuple(DSTS))
        if extra_swap is not None:
            P.op('dve', lambda e: e.tensor_copy(out=extra_swap, in_=self.t_ps[:, nt * 128:2 * nt * 128]), reads=(self.TPS,) + tuple(DSTS), writes=tuple(DSTS))

    def kv_phase(self, li, g):
        P = self.P
        rc = self.rowc
        hd = g['hd']
        k0, kc = g['k']
        v0, vc = g['v']
        nv = g['nv']
        wk, WK = self.load_wunit(self.w_in_d[li][:, k0:k0 + kc], kc)
        wv, WV = self.load_wunit(self.w_in_d[li][:, v0:v0 + vc], vc)
        gain = rc[:, RC[g['kg']]:RC[g['kg']] + hd]
        batches = [[0, 1, 2, 3], [4, 5, 6, 7], [8, 9, 10, 11], [12, 13, 14, 15], [16, 17]]
        for tiles in batches:
            nt = len(tiles)
            tt0 = tiles[0]
            pair = self.proj_batch(wk, WK, kc, tiles)
            pairv = self.proj_batch(wv, WV, vc, tiles)
            self.norm_rope_batch(pair, kc, hd, gain, self.ROWC, g['rope'], tiles)
            dsts = [self.KT[tt] for tt in tiles]
            if g['kind'] == 'B':
                self.transpose_batch(kc, nt, lambda c, w: self.kt[0:w, 0, tt0 * 128:(tt0 + nt) * 128], dsts,
                                     extra_swap=self.kt[:, 1, tt0 * 128:(tt0 + nt) * 128])
            else:
                self.transpose_batch(kc, nt, lambda c, w: self.kt[0:w, c, tt0 * 128:(tt0 + nt) * 128], dsts)
            for (b, ntb, src, PS) in self.bank_views(pairv, nt, vc):
                t0 = tt0 + 2 * b
                vdst = self.v[:, t0:t0 + ntb, 0:nv * 65].rearrange("p t (h c) -> p t h c", c=65)[:, :, :, 0:64]
                P.op('act', lambda e, src=src, vdst=vdst: e.activation(out=vdst, in_=src.rearrange("p t (h c) -> p t h c", c=64), func=AF.Copy),
                     reads=(PS,), writes=tuple(self.V[t0 + j] for j in range(ntb)))
        if 'kt' in self.dbg and li == 0 and g['name'] == self.dbg_group:
            P.dma_out_multi(self.KT, self.dbg_d['kt'], self.kt[:].rearrange("p k t -> p (k t)"), self.OUT)
            P.dma_out_multi(self.V, self.dbg_d['v'], self.v[:].rearrange("p t c -> p (t c)"), self.OUT)

    def load_q_unit(self, li, g, c0, n):
        P = self.P
        if g['kind'] != 'B':
            return self.load_wunit(self.w_in_d[li][:, g['q'][0] + c0:g['q'][0] + c0 + n], n)
        si = self.ws_i; self.ws_i = (si + 1) % 2
        ui = self.wu_i; self.wu_i = (ui + 1) % 2
        ws, WS, wu, WU = self.ws[si], self.WS[si], self.wu[ui], self.WU[ui]
        q0, qc = g['q']
        j = c0 // 128
        for kv in range(2):
            hsrc = q0 + (3 * kv + j) * 64
            P.dma_in(WS, ws[:, :, kv * 64:(kv + 1) * 64], self.w_in_d[li][:, hsrc:hsrc + 64].rearrange("(k p) n -> p k n", p=128))
        P.op('pool', lambda e: e.tensor_copy(out=wu[:, :, 0:n], in_=ws[:, :, 0:n]), reads=(WS,), writes=(WU,))
        return wu, WU

    def col_units(self, total):
        out = []
        c = 0
        while c < total:
            n = min(256, total - c)
            out.append((c, n))
            c += n
        return out

    def load_wo(self, li, g, vv):
        P = self.P
        for half in range(2):
            mp2, MP2 = self.m_ps[half], self.M[half]
            for kk in range(4):
                k = half * 4 + kk
                for hl in range(2):
                    P.op('dve', lambda e, k=k, hl=hl: e.tensor_scalar(out=self.dhl[:, hl, :], in0=self.identb[:], scalar1=self.ghl[:, vv, hl, k:k + 1], scalar2=None, op0=ALU.mult),
                         reads=(self.IDENTB, self.GHL), writes=(self.DHL,))
                for hl in range(2):
                    P.mm(lambda e, kk=kk, mp2=mp2, hl=hl: e.matmul(mp2[:, kk * 128:(kk + 1) * 128], lhsT=self.onesb[:], rhs=self.dhl[:, hl, :],
                                                                   start=(kk == 0 and hl == 0), stop=(kk == 3 and hl == 1)),
                         reads=(self.ONESB, self.DHL), writes=(MP2,), signal=(hl == 1))
        r0, rows = g['wo']
        nch = (rows + 127) // 128
        for c in range(nch):
            w = min(128, rows - c * 128)
            si = self.ws_i; self.ws_i = (si + 1) % 2
            ws, WS = self.ws[si], self.WS[si]
            wsf = ws[:].rearrange("p k n -> p (k n)")
            P.dma_in(WS, wsf[0:w, 0:D], self.w_out_d[li][r0 + c * 128:r0 + c * 128 + w, :])
            for half in range(2):
                P.op('dve', lambda e, c=c, w=w, half=half, wsf=wsf: e.tensor_tensor(out=self.wo[0:w, c, half * 512:(half + 1) * 512], in0=wsf[0:w, half * 512:(half + 1) * 512],
                                                                                   in1=self.m_ps[half][0:w, :], op=ALU.mult),
                     reads=(WS, self.M[half]), writes=(self.WO,))

    def load_bias_part(self, li, h, col0, ncols, dst, DST):
        P = self.P
        c = 0
        while c < ncols:
            w = min(1024, ncols - c)
            si = self.ws_i; self.ws_i = (si + 1) % 2
            ws, WS = self.ws[si], self.WS[si]
            wsf = ws[:].rearrange("p k n -> p (k n)")
            P.dma_in(WS, wsf[:, 0:w], self.bias_d[li, h][:, col0 + c:col0 + c + w])
            P.op('pool', lambda e, c=c, w=w, wsf=wsf: e.tensor_copy(out=dst[:, c:c + w], in_=wsf[:, 0:w]), reads=(WS,), writes=(DST,))
            c += w

    def attention(self, jobs):
        P = self.P
        flat = []
        for job in jobs:
            nb = len(job['blocks'])
            oi = self.o_i; self.o_i = (oi + 1) % 2
            job['o'] = (self.o_ps[oi], self.O[oi])
            for bi, b in enumerate(job['blocks']):
                flat.append((job, bi, nb, b))
        n = len(flat)
        st = [None] * n
        self.inter_every = max(1, min(3, n // 40))

        def prep_mask(job):
            if 'qm' in job:
                return
            mi = self.qm_i; self.qm_i = (mi + 1) % 3
            job['qm'] = (self.qm[mi], self.QM[mi])
            qm_, QM_ = job['qm']
            mc = job['mcol']
            nq_ = job['blocks'][0]['nq']
            P.op('pool', lambda e: e.tensor_scalar(out=qm_[:, 0:nq_], in0=job['qsrc'], scalar1=self.hmask[:, mc:mc + 1], scalar2=1.0, op0=ALU.mult, op1=ALU.mult),
                 reads=(self.QT, self.HMASK), writes=(QM_,))

        def emit_qk(i):
            job, bi, nb, b = flat[i]
            si = self.s_i; self.s_i = (si + 1) % 3
            sp, SP_ = self.s_ps[si], self.S[si]
            nq = b['nq']
            bias = b.get('bias') or []
            if bi == 0:
                prep_mask(job)
                ji = jobs.index(job)
                if ji + 1 < len(jobs):
                    prep_mask(jobs[ji + 1])
            qm_, QM_ = job['qm']
            P.mm(lambda e: e.matmul(sp[:, 0:nq], lhsT=b['kt'], rhs=qm_[:, 0:nq], start=True, stop=(len(bias) == 0)),
                 reads=(b['KT'], QM_), writes=(SP_,), signal=(len(bias) == 0))
            for xi, (c0, c1, uap) in enumerate(bias):
                last = xi == len(bias) - 1
                P.mm(lambda e, c0=c0, c1=c1, uap=uap, last=last: e.matmul(sp[:, c0:c1], lhsT=self.identb[:], rhs=uap, start=False, stop=last),
                     reads=(self.IDENTB,) + tuple(b['ULT']), writes=(SP_,), signal=last)
            pi = self.pt_i; self.pt_i = (pi + 1) % 3
            pt, PT_ = self.pt[pi], self.PT[pi]
            P.op('act', lambda e: e.activation(out=pt[:, 0:nq], in_=sp[:, 0:nq], func=AF.Exp, scale=job['scale'], bias=SOFTMAX_SHIFT),
                 reads=(SP_,), writes=(PT_,))
            st[i] = (pt, PT_)

        def emit_pv(i):
            job, bi, nb, b = flat[i]
            pt, PT_ = st[i]
            op_, O_ = job['o']
            nsub = b['nq'] // 128
            for j in range(nsub):
                P.mm(lambda e, j=j: e.matmul(op_[:, j * 65:(j + 1) * 65], lhsT=pt[:, j * 128:(j + 1) * 128], rhs=b['v'],
                                             start=(bi == 0 and j == 0), stop=(bi == nb - 1 and j == nsub - 1)),
                     reads=(PT_, b['V']), writes=(O_,), signal=(j == nsub - 1))
            if bi == nb - 1:
                job['finish'](op_, O_)

        for i in range(min(2, n)):
            emit_qk(i)
        for i in range(n):
            if i + 2 < n:
                emit_qk(i + 2)
            emit_pv(i)
            if self.inter_gen is not None and i % self.inter_every == self.inter_every - 1:
                try:
                    next(self.inter_gen)
                except StopIteration:
                    self.inter_gen = None

    def q_proj_gen(self, li, g, ch, qT, QT, force_pair=None):
        rc = self.rowc
        hd = g['hd']
        kind = g['kind']
        q0, qcols = g['q']
        nsub = 4 if ch < 4 else 2
        nq = nsub * 128
        tts = [ch * 4 + i for i in range(4)] if ch < 4 else [16, 17]
        gain = rc[:, RC[g['qg']]:RC[g['qg']] + hd] if kind != 'C' else self.gqc[:]
        GAINT = self.ROWC if kind != 'C' else self.GQC
        for (c0, n) in (self.col_units(qcols) if kind != 'B' else [(0, 128), (128, 128), (256, 128)]):
            wq, WQ = self.load_q_unit(li, g, c0, n)
            yield
            pair = self.proj_batch(wq, WQ, n, tts, force_pair=force_pair)
            yield
            self.norm_rope_batch(pair, n, hd, gain, GAINT, g['rope'], tts)
            yield
            cbase = c0 // 128
            self.transpose_batch(n, nsub, lambda c, w, cbase=cbase: qT[0:w, cbase + c, 0:nq], [QT])
            yield

    def g_proj_gen(self, li, g, ch):
        P = self.P
        g0, gcols = g['g']
        nsub = 4 if ch < 4 else 2
        tts = [ch * 4 + i for i in range(4)] if ch < 4 else [16, 17]
        for (c0, n) in self.col_units(gcols):
            wg, WG = self.load_wunit(self.w_in_d[li][:, g0 + c0:g0 + c0 + n], n)
            yield
            pair = self.proj_batch(wg, WG, n, tts, force_pair=0)
            yield
            for (b, ntb, src, PS) in self.bank_views(pair, nsub, n):
                P.op('act', lambda e, b=b, ntb=ntb, src=src, n=n: e.activation(out=self.qf[:, 2 * b:2 * b + ntb, 0:n], in_=src, func=AF.Tanh, scale=0.5), reads=(PS,), writes=(self.QF,))
                P.op('dve', lambda e, b=b, ntb=ntb, src=src, c0=c0, n=n: e.scalar_tensor_tensor(out=self.g_sb[:, 2 * b:2 * b + ntb, c0:c0 + n], in0=self.qf[:, 2 * b:2 * b + ntb, 0:n], scalar=1.0, op0=ALU.add, in1=src, op1=ALU.mult),
                     reads=(self.QF, PS), writes=(self.G,))
            yield

    def qg_chunk(self, li, g, ch, is_last_layer, qbuf, next_gen):
        P = self.P
        self.qT, self.QT = qbuf
        self.inter_gen = next_gen
        rc = self.rowc
        hd = g['hd']
        kind = g['kind']
        q0, qcols = g['q']
        g0, gcols = g['g']
        nsub = 4 if ch < 4 else 2
        nq = nsub * 128
        tts = [ch * 4 + i for i in range(4)] if ch < 4 else [16, 17]
        gain = rc[:, RC[g['qg']]:RC[g['qg']] + hd] if kind != 'C' else self.gqc[:]
        GAINT = self.ROWC if kind != 'C' else self.GQC
        if getattr(self, 'stop_after', None) == 'gproj':
            return
        jobs = []
        kblocks_all = list(range(NT)) if ch < 4 else [16, 17]
        if kind == 'A':
            scale = 32 ** -0.5
            shift = self.lyr[:, 1:2]
            for s in range(8):
                h, m = s // 2, s % 2
                chunk, poff = s // 4, 32 * (s % 4)
                blocks = [dict(kt=self.kt[:, chunk, kb * 128:(kb + 1) * 128],
                               v=self.v[:, kb, h * 65:(h + 1) * 65], KT=self.KT[kb], V=self.V[kb], nq=nq) for kb in kblocks_all]
                jq = dict(qsrc=self.qT[:, chunk, 0:nq], mcol=s % 4)

                def fin(op_, O_, h=h, m=m, nsub=nsub):
                    o3 = op_[:, 0:nsub * 65].rearrange("p (j c) -> p j c", c=65)
                    sm = self.small2
                    P.op('dve', lambda e: e.reciprocal(out=sm[:, 0:nsub], in_=o3[:, :, 64]), reads=(O_, self.SMALL2), writes=(self.SMALL2,))
                    P.op('dve', lambda e: e.tensor_tensor(out=self.o01[m][:, 0:nsub, :], in0=o3[:, :, 0:64], in1=sm[:, 0:nsub].unsqueeze(2).to_broadcast([128, nsub, 64]), op=ALU.mult),
                         reads=(O_, self.SMALL2), writes=(self.O01[m],))
                    if m == 1:
                        P.op('dve', lambda e: e.scalar_tensor_tensor(out=self.otok[:, 0:nsub, h * 64:(h + 1) * 64], in0=self.o01[1][:, 0:nsub, :], scalar=self.lyr[:, 0:1], op0=ALU.mult,
                                                                     in1=self.o01[0][:, 0:nsub, :], op1=ALU.add),
                             reads=(self.O01[0], self.O01[1], self.LYR), writes=(self.OTOK,))
                jobs.append(dict(blocks=blocks, scale=scale, shift=shift, finish=fin, **jq))
        else:
            scale = 64 ** -0.5 if kind == 'B' else 1.0
            shift = self.lyr[:, 2:3] if kind == 'B' else self.lyr[:, 3:4]
            for hh in range(g['nh']):
                if kind == 'B':
                    kvh = hh // 3
                    pos = 2 * (hh % 3) + kvh
                    qchunk, poff = pos // 2, 64 * (pos % 2)
                    kchunk = 0 if poff == 64 * kvh else 1
                    vh = kvh
                else:
                    qchunk, poff = hh // 2, 64 * (hh % 2)
                    kchunk = qchunk
                    vh = hh
                if kind == 'C' and ch < 4:
                    blist = c_key_blocks(ch) + [(16, None), (17, None)]
                else:
                    blist = [(kb, None) for kb in kblocks_all]
                blocks = []
                for kb, segs in blist:
                    b = dict(kt=self.kt[:, kchunk, kb * 128:(kb + 1) * 128],
                             v=self.v[:, kb, vh * 65:(vh + 1) * 65], KT=self.KT[kb], V=self.V[kb], nq=nq)
                    if segs:
                        bl = []
                        for (c0, c1, tab, ms) in segs:
                            if tab == 'I':
                                bl.append((c0, c1, self.ubi[:, hh, ms * 64:ms * 64 + (c1 - c0)]))
                            else:
                                bl.append((c0, c1, self.ubfus[hh % 2][:, ms * 64:ms * 64 + (c1 - c0)]))
                        b['bias'] = bl
                        b['ULT'] = (self.UBI[hh], self.UBFUS[hh % 2])
                    blocks.append(b)

                def fin(op_, O_, hh=hh, nsub=nsub):
                    o3 = op_[:, 0:nsub * 65].rearrange("p (j c) -> p j c", c=65)
                    sm = self.small2
                    P.op('dve', lambda e: e.reciprocal(out=sm[:, 0:nsub], in_=o3[:, :, 64]), reads=(O_, self.SMALL2), writes=(self.SMALL2,))
                    P.op('dve', lambda e: e.tensor_tensor(out=self.otok[:, 0:nsub, hh * 64:(hh + 1) * 64], in0=o3[:, :, 0:64], in1=sm[:, 0:nsub].unsqueeze(2).to_broadcast([128, nsub, 64]), op=ALU.mult),
                         reads=(O_, self.SMALL2), writes=(self.OTOK,))
                jobs.append(dict(blocks=blocks, scale=scale, shift=shift, finish=fin, head=hh, qsrc=self.qT[:, qchunk, 0:nq], mcol=4 + poff // 64))
        if kind == 'C' and ch < 4:
            if ch in (0, 3):
                for hh in range(2):
                    self.load_bias_part(li, g['h0'] + hh, UINT_M * 64, UFULL_M * 64, self.ubfus[hh], self.UBFUS[hh])
                fin0 = jobs[0]['finish']

                def fin0_then_load(op_, O_, fin0=fin0):
                    fin0(op_, O_)
                    self.load_bias_part(li, g['h0'] + 2, UINT_M * 64, UFULL_M * 64, self.ubfus[0], self.UBFUS[0])
                jobs[0]['finish'] = fin0_then_load
            self.attention(jobs)
        else:
            self.attention(jobs)
        if 'qt' in self.dbg and li == 0 and g['name'] == self.dbg_group and ch == self.dbg_chunk:
            P.dma_out(self.QT, self.dbg_d['qt'], self.qT[:].rearrange("p c t -> p (c t)"), self.OUT)
            P.dma_out(self.OTOK, self.dbg_d['otok'], self.otok[:].rearrange("p c t -> p (c t)"), self.OUT)
        if self.inter_gen is not None:
            for _ in self.inter_gen:
                pass
            self.inter_gen = None
        if getattr(self, 'stop_after', None) == 'attn':
            return
        if kind == 'A':
            sm = self.small
            m = nsub * 4
            o4 = self.otok[:, 0:nsub, 0:256].rearrange("p t (h d) -> p t h d", d=64)
            P.op('act', lambda e: e.activation(out=self.sq[:, 0:nsub, 0:256], in_=self.otok[:, 0:nsub, 0:256], func=AF.Square), reads=(self.OTOK,), writes=(self.SQ,))
            P.op('dve', lambda e: e.tensor_reduce(out=sm[:, 200:200 + m].rearrange("p (t h) -> p t h", h=4), in_=self.sq[:, 0:nsub, 0:256].rearrange("p t (h d) -> p t h d", d=64), op=ALU.add, axis=AX.X),
                 reads=(self.SQ, self.SMALL), writes=(self.SMALL,))
            P.op('act', lambda e: e.activation(out=sm[:, 240:240 + m], in_=sm[:, 200:200 + m], func=AF.Ln, scale=1.0 / 64, bias=EPS), reads=(self.SMALL,), writes=(self.SMALL,))
            P.op('act', lambda e: e.activation(out=sm[:, 280:280 + m], in_=sm[:, 240:240 + m], func=AF.Exp, scale=-0.5), reads=(self.SMALL,), writes=(self.SMALL,))
            P.op('dve', lambda e: e.tensor_tensor(out=o4, in0=o4, in1=sm[:, 280:280 + m].rearrange("p (t h) -> p t h", h=4).unsqueeze(3).to_broadcast([128, nsub, 4, 64]), op=ALU.mult),
                 reads=(self.OTOK, self.SMALL), writes=(self.OTOK,))
            P.op('dve', lambda e: e.tensor_tensor(out=o4, in0=o4, in1=self.gains[:].unsqueeze(1).unsqueeze(1).to_broadcast([128, nsub, 4, 64]), op=ALU.mult),
                 reads=(self.OTOK, self.GAINS), writes=(self.OTOK,))
        P.op('dve', lambda e: e.scalar_tensor_tensor(out=self.ytok[:, 0:nsub, 0:gcols], in0=self.otok[:, 0:nsub, 0:gcols], scalar=0.5, op0=ALU.mult, in1=self.g_sb[:, 0:nsub, 0:gcols], op1=ALU.mult),
             reads=(self.OTOK, self.G), writes=(self.YTOK,))
        if 'ytok' in self.dbg and li == 0 and g['name'] == self.dbg_group and ch == self.dbg_chunk:
            P.dma_out(self.YTOK, self.dbg_d['ytok'], self.ytok[:].rearrange("p c t -> p (c t)"), self.OUT)
            P.dma_out(self.G, self.dbg_d['g'], self.g_sb[:].rearrange("p c t -> p (c t)"), self.OUT)
        return self.post2_gen(g, ch, (self.qT, self.QT))

    def post2_gen(self, g, ch, ybuf):
        P = self.P
        g0, gcols = g['g']
        nsub = 4 if ch < 4 else 2
        tts = [ch * 4 + i for i in range(4)] if ch < 4 else [16, 17]
        nch = (gcols + 127) // 128
        yT, YT = ybuf
        for i in range(nsub):
            self.transpose_to(self.ytok[:, i, :], self.YTOK, gcols, lambda c, w, i=i, yT=yT: yT[0:w, c, i * 128:(i + 1) * 128], YT)
            yield
        for i, tt in enumerate(tts):
            for half in range(2):
                mp, MP = self.next_m()
                for c in range(nch):
                    w = min(128, gcols - c * 128)
                    P.mm(lambda e, c=c, w=w, i=i, half=half, mp=mp, yT=yT: e.matmul(mp[:], lhsT=yT[0:w, c, i * 128:(i + 1) * 128], rhs=self.wo[0:w, c, half * 512:(half + 1) * 512],
                                                                                  start=(c == 0), stop=(c == nch - 1)),
                         reads=(YT, self.WO), writes=(MP,), signal=(c == nch - 1))
                P.op('dve', lambda e, tt=tt, half=half, mp=mp: e.tensor_tensor(out=self.x_sb[:, tt, half * 512:(half + 1) * 512], in0=mp[:], in1=self.x_sb[:, tt, half * 512:(half + 1) * 512], op=ALU.add),
                     reads=(MP, self.X[tt]), writes=(self.X[tt],))
                yield

    def layer(self, li, is_last):
        l_global = self.layers[li]
        self.P.dma_in(self.ROWC, self.rowc[:], self.rowc_d[:, li * RCW:(li + 1) * RCW])
        sa_ = getattr(self, 'stop_after', None)
        self.adaln(li)
        if sa_ == 'adaln':
            return
        self.layer_consts(li, l_global)
        if sa_ == 'consts':
            return
        self.build_hT(li)
        if sa_ == 'hT':
            return
        groups = GROUPS if not getattr(self, 'only_groups', None) else [g for g in GROUPS if g['name'] in self.only_groups]
        for g in groups:
            self.kv_phase(li, g)
            if sa_ == 'kv':
                return
            chunks = [0, 1, 2, 3] + ([] if is_last else [4])
            if getattr(self, 'only_chunks', None) is not None:
                chunks = [c for c in chunks if c in self.only_chunks]
            if g['kind'] == 'C':
                for hh in range(3):
                    self.load_bias_part(li, g['h0'] + hh, 0, UINT_M * 64, self.ubi[:, hh, :], self.UBI[hh])
            for _ in self.q_proj_gen(li, g, chunks[0], self.qTs[0], self.QTs[0]):
                pass
            pending = None
            for idx, ch in enumerate(chunks):
                if ch == chunks[0] or ch == 4:
                    if pending is not None:
                        for _ in pending:
                            pass
                        pending = None
                    self.load_wo(li, g, 0 if ch < 4 else 1)
                if sa_ == 'wo':
                    return
                gens = [self.g_proj_gen(li, g, ch)]
                if pending is not None:
                    gens.append(pending)
                if idx + 1 < len(chunks):
                    nb = (idx + 1) % 2
                    gens.append(self.q_proj_gen(li, g, chunks[idx + 1], self.qTs[nb], self.QTs[nb], force_pair=0))
                pending = self.qg_chunk(li, g, ch, is_last, (self.qTs[idx % 2], self.QTs[idx % 2]), itertools.chain(*gens))
            if pending is not None:
                for _ in pending:
                    pass

    def epilogue(self):
        P = self.P
        for tt in range(NLT):
            P.dma_out(self.X[tt], self.y_d[tt * 128:(tt + 1) * 128, :], self.x_sb[:, tt, :], self.OUT)
        if self.emit_ctx:
            for tt in range(NLT, NT):
                P.dma_out(self.X[tt], self.ctxo_d[(tt - NLT) * 128:(tt - NLT + 1) * 128, :], self.x_sb[:, tt, :], self.OUT)
        P.wait_dma_done(self.OUT)


def make_builder(layers, last_flags, emit_ctx, dbg=(), **opts):
    b = Builder(layers, last_flags, emit_ctx, dbg)
    for k, v in opts.items():
        setattr(b, k, v)
    return b


def host_consts(inputs, layers):
    nl = len(layers)
    colc = np.zeros((128, nl, 32), np.float32)
    rowc = np.zeros((128, nl, RCW), np.float32)
    for i, l in enumerate(layers):
        colc[:, i, 0:8] = inputs['norm_g'][l].reshape(8, 128).T
        colc[:, i, 8:32] = inputs['b_ada'][l].reshape(24, 128).T
        for nm, key, n in (('dqn', 'diff_q_norm', 32), ('dkn', 'diff_k_norm', 32), ('subln', 'diff_subln', 64),
                           ('gqn', 'gqa_q_norm', 64), ('gkn', 'gqa_k_norm', 64), ('nqn', 'nat_q_norm', 64), ('nkn', 'nat_k_norm', 64),
                           ('lq1', 'lambda_q1', 32), ('lk1', 'lambda_k1', 32), ('lq2', 'lambda_q2', 32), ('lk2', 'lambda_k2', 32)):
            rowc[:, i, RC[nm]:RC[nm] + n] = inputs[key][l][None, :]
    return colc.reshape(128, nl * 32), rowc.reshape(128, nl * RCW)


def core_inputs(inputs, b, layers, x_b, ctx_b, shared):
    cm = np.zeros((128, 8, 2), np.float32)
    cm[:, :, 0] = inputs['c'][b].reshape(8, 128).T
    cm[:, :, 1] = inputs['c_ctx'].reshape(8, 128).T
    d = dict(shared)
    d.update(x=np.ascontiguousarray(x_b), ctx=np.ascontiguousarray(ctx_b), cmat=cm.reshape(128, 16))
    return d


def shared_inputs(inputs, layers):
    colc, rowc = host_consts(inputs, layers)
    ls = list(layers)
    return dict(
        w_ada=np.ascontiguousarray(inputs['w_ada'][ls]), w_in=np.ascontiguousarray(inputs['w_in'][ls]),
        w_out=np.ascontiguousarray(inputs['w_out'][ls]), colc=colc, rowc=rowc,
        rope=rope_tables().reshape(128, NLT * 96), ident=np.eye(128, dtype=np.float32), hmask=head_masks(),
        biasu=build_bias_tables(np.asarray(inputs['nat_rpb'])[ls]))


_NC_CACHE = {}


def kernel(**inputs):
    inputs = {k: np.asarray(v) for k, v in inputs.items()}
    B = inputs['x'].shape[0]
    layers = list(range(DEPTH))
    key = ('full',)
    if key not in _NC_CACHE:
        bld = make_builder(layers, [l == DEPTH - 1 for l in layers], emit_ctx=False)
        _NC_CACHE[key] = bld.build()
    nc = _NC_CACHE[key]
    shared = shared_inputs(inputs, layers)
    in_maps = [core_inputs(inputs, b, layers, inputs['x'][b], inputs['ctx'][b], shared) for b in range(B)]
    res = run_bass_kernel_spmd(nc, in_maps, core_ids=list(range(B)))
    return np.stack([r["y"] for r in res.results], axis=0).astype(np.float32)
```

```python
import math
import itertools
import numpy as np
import ml_dtypes
import concourse.bass as bass
import concourse.mybir as mybir
from concourse.bass_utils import run_bass_kernel_spmd
from contextlib import ExitStack

F32 = mybir.dt.float32
BF16 = mybir.dt.bfloat16
AF = mybir.ActivationFunctionType
ALU = mybir.AluOpType
AX = mybir.AxisListType

D = 1024
SEQ = 2048
NCTX = 256
TOK = SEQ + NCTX
NT = TOK // 128
NLT = SEQ // 128
DEPTH = 4
IN_W = 3584
EPS = 1e-6
NEG = -30000.0
SOFTMAX_SHIFT = -12.0
GRID_W = 64
UINT_M = 22
UFULL_M = 15
UW = (UINT_M + UFULL_M) * 64
UINT_C = 10
UFULL_C = 7

RC = dict(dqn=0, dkn=32, subln=64, gqn=128, gkn=192, nqn=256, nkn=320, lq1=384, lk1=416, lq2=448, lk2=480)
RCW = 512

GROUPS = [
    dict(name='A', kind='A', hd=32, nh=8, nv=4, q=(0, 256), k=(256, 256), v=(512, 256), g=(768, 256),
         wo=(0, 256), rope='A', qg='dqn', kg='dkn'),
    dict(name='B', kind='B', hd=64, nh=6, nv=2, q=(1024, 384), k=(1408, 128), v=(1536, 128), g=(1664, 384),
         wo=(256, 384), rope='B', qg='gqn', kg='gkn'),
    dict(name='C1', kind='C', hd=64, nh=3, nv=3, q=(2048, 192), k=(2432, 192), v=(2816, 192), g=(3200, 192),
         wo=(640, 192), rope=None, qg='nqn', kg='nkn', h0=0),
    dict(name='C2', kind='C', hd=64, nh=3, nv=3, q=(2240, 192), k=(2624, 192), v=(3008, 192), g=(3392, 192),
         wo=(832, 192), rope=None, qg='nqn', kg='nkn', h0=3),
]


class LT:
    __slots__ = ('name', 'w', 'r', 'dsem', 'dtot')

    def __init__(self, name):
        self.name = name
        self.w = None
        self.r = {}
        self.dsem = None
        self.dtot = 0


class Prog:
    def __init__(self, nc, es):
        self.nc = nc
        self.es = es
        self.engs = ('pe', 'act', 'dve', 'pool', 'sp')
        self.sem = {k: es.enter_context(nc.semaphore('s_' + k)) for k in self.engs}
        self.cnt = {k: 0 for k in self.engs}
        self.seen = {k: {} for k in self.engs}
        self.q = {k: [] for k in self.engs}
        self.nops = 0
        self.nwaits = 0

    def sbuf(self, name, shape, dt):
        return self.es.enter_context(self.nc.sbuf_tensor('sb_' + name, shape, dt))

    def psum(self, name, shape, dt):
        return self.es.enter_context(self.nc.psum_tensor('ps_' + name, shape, dt))

    def _wait(self, e, dep):
        key, sem, val = dep
        if key == 'pe' and e == 'pe':
            return
        if self.seen[e].get(key, 0) >= val:
            return
        self.q[e].append(('w', sem, val))
        self.seen[e][key] = val
        self.nwaits += 1

    def deps(self, e, reads, writes):
        for t in reads:
            if t.w is not None:
                self._wait(e, t.w)
        for t in writes:
            if t.w is not None:
                self._wait(e, t.w)
            for d in t.r.values():
                self._wait(e, d)

    def op(self, e, fn, reads=(), writes=()):
        self.deps(e, reads, writes)
        self.cnt[e] += 1
        self.q[e].append(('o', fn, self.sem[e], 1))
        d = (e, self.sem[e], self.cnt[e])
        for t in reads:
            t.r[e] = d
        for t in writes:
            t.w = d
            t.r = {}
        self.nops += 1

    def mm(self, fn, reads=(), writes=(), signal=True):
        e = 'pe'
        self.deps(e, reads, writes)
        if signal:
            self.cnt[e] += 1
            self.q[e].append(('o', fn, self.sem[e], 1))
            d = (e, self.sem[e], self.cnt[e])
        else:
            self.q[e].append(('o', fn, None, 0))
            d = (e, self.sem[e], self.cnt[e] + 1)
        for t in reads:
            t.r[e] = d
        for t in writes:
            t.w = d
            t.r = {}
        self.nops += 1

    def dma_in(self, t, out_ap, in_ap, q='sp'):
        if t.dsem is None:
            t.dsem = self.es.enter_context(self.nc.semaphore('d_' + t.name))
        self.deps(q, (), (t,))
        self.q[q].append(('o', lambda e, o=out_ap, i=in_ap: e.dma_start(out=o, in_=i), t.dsem, 16))
        t.dtot += 16
        t.w = (('dma', t.name), t.dsem, t.dtot)
        t.r = {}
        self.nops += 1

    def dma_out(self, t, out_ap, in_ap, h, q='sp'):
        if h.dsem is None:
            h.dsem = self.es.enter_context(self.nc.semaphore('d_' + h.name))
        self.deps(q, (t,), ())
        self.q[q].append(('o', lambda e, o=out_ap, i=in_ap: e.dma_start(out=o, in_=i), h.dsem, 16))
        h.dtot += 16
        t.r[('dma', h.name)] = (('dma', h.name), h.dsem, h.dtot)
        self.nops += 1

    def dma_out_multi(self, tiles, out_ap, in_ap, h, q='sp'):
        if h.dsem is None:
            h.dsem = self.es.enter_context(self.nc.semaphore('d_' + h.name))
        self.deps(q, tuple(tiles), ())
        self.q[q].append(('o', lambda e, o=out_ap, i=in_ap: e.dma_start(out=o, in_=i), h.dsem, 16))
        h.dtot += 16
        for t in tiles:
            t.r[('dma', h.name)] = (('dma', h.name), h.dsem, h.dtot)
        self.nops += 1

    def wait_dma_done(self, h, q='sp'):
        self.q[q].append(('w', h.dsem, h.dtot))

    def emit(self):
        def replay(lst):
            def f(e):
                for it in lst:
                    if it[0] == 'w':
                        e.wait_ge(it[1], it[2])
                    else:
                        ins = it[1](e)
                        if it[2] is not None:
                            ins.then_inc(it[2], it[3])
            return f
        with self.nc.Block() as block:
            block.sync(replay(self.q['sp']))
            block.tensor(replay(self.q['pe']))
            block.scalar(replay(self.q['act']))
            block.vector(replay(self.q['dve']))
            block.gpsimd(replay(self.q['pool']))


def c_key_blocks(j):
    out = []
    qr0 = 8 * j
    if j == 0:
        kbs = range(0, 6)
    elif j == 3:
        kbs = range(10, 16)
    else:
        kbs = range(4 * j - 2, 4 * j + 6)
    for kb in kbs:
        kr0 = 2 * kb
        e = kr0 - qr0
        if j == 0:
            segs = []
            if kb < 4:
                segs.append((0, 320, 'F', UFULL_C - e))
            else:
                segs.append((0, 320, 'I', 0))
            segs.append((320, 512, 'I', UINT_C - e + 5))
        elif j == 3:
            segs = [(0, 256, 'I', UINT_C - e)]
            if kb >= 12:
                segs.append((256, 512, 'F', UFULL_C - e + 4))
            else:
                segs.append((256, 512, 'I', 0))
        else:
            segs = [(0, 512, 'I', UINT_C - e)]
        out.append((kb, segs))
    return out


def build_bias_tables(rpb):
    Lr = rpb.shape[0]
    kc = np.arange(64)[:, None]
    qc = np.arange(64)[None, :]
    c0 = np.clip(qc - 8, 0, 48)
    colvalid = (kc >= c0) & (kc < c0 + 16)
    dcidx = np.clip(kc - qc + 15, 0, 30)
    U = np.full((Lr, 6, 2, 64, UINT_M + UFULL_M, 64), NEG, np.float32)
    for half in range(2):
        for m in range(UINT_M):
            dr = UINT_C + half - m
            if -4 <= dr <= 3:
                vals = rpb[:, :, dr + 7, :][:, :, dcidx]
                U[:, :, half, :, m, :] = np.where(colvalid[None, None], vals, np.float32(NEG))
        for m in range(UFULL_M):
            dr = UFULL_C + half - m
            if -7 <= dr <= 7:
                vals = rpb[:, :, dr + 7, :][:, :, dcidx]
                U[:, :, half, :, UINT_M + m, :] = np.where(colvalid[None, None], vals, np.float32(NEG))
    return np.ascontiguousarray(U.reshape(Lr, 6, 128, UW))


def head_masks():
    m = np.zeros((128, 8), np.float32)
    for i in range(4):
        m[32 * i:32 * i + 32, i] = 1.0
    m[0:64, 4] = 1.0
    m[64:128, 5] = 1.0
    return m


def rope_tables():
    t = np.arange(SEQ)
    row = (t // GRID_W).astype(np.float32)
    col = (t % GRID_W).astype(np.float32)
    outs = []
    for d in (64, 32):
        nf = d // 4
        inv = (10000.0 ** (-np.arange(nf, dtype=np.float32) / nf)).astype(np.float32)
        ang = np.concatenate([row[:, None] * inv, col[:, None] * inv], axis=-1).astype(np.float32)
        outs.append(np.cos(ang).astype(np.float32))
        outs.append(np.sin(ang).astype(np.float32))
    tab = np.concatenate(outs, axis=-1)
    return np.ascontiguousarray(tab.reshape(NLT, 128, 96).transpose(1, 0, 2))


class Builder:
    def __init__(self, layers, last_layer_flags, emit_ctx, dbg=()):
        self.layers = layers
        self.last_flags = last_layer_flags
        self.emit_ctx = emit_ctx
        self.dbg = set(dbg)
        self.nc = bass.Bass("TRN2", target_bir_lowering=False)

    def dram_in(self, name, shape):
        return self.nc.dram_tensor(name, list(shape), F32, kind="ExternalInput").ap()

    def dram_out(self, name, shape, dt=F32):
        return self.nc.dram_tensor(name, list(shape), dt, kind="ExternalOutput").ap()

    def build(self):
        nc = self.nc
        with ExitStack() as es:
            self.P = P = Prog(nc, es)
            nl = len(self.layers)
            self.x_d = self.dram_in("x", [SEQ, D])
            self.ctx_d = self.dram_in("ctx", [NCTX, D])
            self.cmat_d = self.dram_in("cmat", [128, 16])
            self.w_ada_d = self.dram_in("w_ada", [nl, D, 3 * D])
            self.w_in_d = self.dram_in("w_in", [nl, D, IN_W])
            self.w_out_d = self.dram_in("w_out", [nl, D, D])
            self.colc_d = self.dram_in("colc", [128, nl * 32])
            self.rowc_d = self.dram_in("rowc", [128, nl * RCW])
            self.rope_d = self.dram_in("rope", [128, NLT * 96])
            self.ident_d = self.dram_in("ident", [128, 128])
            self.hmask_d = self.dram_in("hmask", [128, 8])
            self.bias_d = self.dram_in("biasu", [nl, 6, 128, UW])
            self.y_d = self.dram_out("y", [SEQ, D])
            if self.emit_ctx:
                self.ctxo_d = self.dram_out("ctx_out", [NCTX, D])
            self.dbg_d = {}
            for name, shape in (('hT', [128, 8 * TOK]), ('kt', [128, 2 * TOK]), ('v', [128, NT * 260]),
                                ('qt', [128, 3 * 512]), ('otok', [128, 4 * 384]), ('ytok', [128, 4 * 384]),
                                ('g', [128, 4 * 384]), ('mod', [128, 48])):
                if name in self.dbg:
                    self.dbg_d[name] = self.dram_out("dbg_" + name, shape, F32 if name in ('otok', 'mod') else BF16)
            self.alloc()
            self.prologue()
            for li, l in enumerate(self.layers):
                self.layer(li, self.last_flags[li])
            self.epilogue()
            P.emit()
        return nc

    def alloc(self):
        P = self.P
        self.x_sb = P.sbuf("x_sb", [128, NT, D], F32)
        self.X = [LT("x%d" % i) for i in range(NT)]
        self.hT = P.sbuf("hT", [128, 8, TOK], BF16)
        self.HT = [LT("hT%d" % i) for i in range(NT)]
        self.kt = P.sbuf("kt", [128, 2, TOK], BF16)
        self.KT = [LT("kt%d" % i) for i in range(NT)]
        self.v = P.sbuf("v", [128, NT, 260], BF16)
        self.V = [LT("v%d" % i) for i in range(NT)]
        self.ws = [P.sbuf("ws%d" % i, [128, 8, 128], F32) for i in range(2)]
        self.WS = [LT("ws%d" % i) for i in range(2)]
        self.wu = [P.sbuf("wu%d" % i, [128, 8, 256], BF16) for i in range(2)]
        self.WU = [LT("wu%d" % i) for i in range(2)]
        self.ws_i = 0
        self.wu_i = 0
        self.wo = P.sbuf("wo", [128, 3, D], BF16)
        self.WO = LT("wo")
        self.qf = P.sbuf("qf", [128, 4, 256], F32); self.QF = LT("qf")
        self.sq = P.sbuf("sq", [128, 4, 256], F32); self.SQ = LT("sq")
        self.qb = P.sbuf("qb", [128, 4, 384], BF16); self.QB = LT("qb")
        self.qm = [P.sbuf("qm%d" % i, [128, 512], BF16) for i in range(3)]; self.QM = [LT("qm%d" % i) for i in range(3)]
        self.qm_i = 0
        self.hmask = P.sbuf("hmask", [128, 8], F32); self.HMASK = LT("hmask")
        self.small2 = P.sbuf("small2", [128, 16], F32); self.SMALL2 = LT("small2")
        self.qTs = [P.sbuf("qT%d" % i, [128, 3, 512], BF16) for i in range(2)]; self.QTs = [LT("qT%d" % i) for i in range(2)]
        self.qT, self.QT = self.qTs[0], self.QTs[0]
        self.g_sb = P.sbuf("g_sb", [128, 4, 384], BF16); self.G = LT("g")
        self.otok = P.sbuf("otok", [128, 4, 384], F32); self.OTOK = LT("otok")
        self.sa = self.otok[:].rearrange("p a b -> p (a b)")[:, 0:1024]; self.SA = self.OTOK
        self.ytok, self.YTOK = self.qb, self.QB
        self.pt = [P.sbuf("pt%d" % i, [128, 512], BF16) for i in range(3)]
        self.PT = [LT("pt%d" % i) for i in range(3)]
        self.pt_i = 0
        self.o01 = [P.sbuf("o01_%d" % i, [128, 4, 64], F32) for i in range(2)]
        self.O01 = [LT("o01_%d" % i) for i in range(2)]
        self.ubi = P.sbuf("ubi", [128, 3, UINT_M * 64], BF16); self.UBI = [LT("ubi%d" % i) for i in range(3)]
        self.ubfus = [P.sbuf("ubfu%d" % i, [128, UFULL_M * 64], BF16) for i in range(2)]
        self.UBFUS = [LT("ubfu%d" % i) for i in range(2)]
        self.small = P.sbuf("small", [128, 320], F32); self.SMALL = LT("small")
        self.rope = P.sbuf("rope", [128, NLT, 96], F32); self.ROPE = LT("rope")
        self.ident = P.sbuf("ident", [128, 128], F32); self.IDENT = LT("ident")
        self.identb = P.sbuf("identb", [128, 128], BF16); self.IDENTB = LT("identb")
        self.onesb = P.sbuf("onesb", [128, 128], BF16); self.ONESB = LT("onesb")
        self.dhl = P.sbuf("dhl", [128, 2, 128], BF16); self.DHL = LT("dhl")
        self.ghl = P.sbuf("ghl", [128, 2, 2, 8], BF16); self.GHL = LT("ghl")
        self.gtmp = P.sbuf("gtmp", [128, 16], F32); self.GTMP = LT("gtmp")
        nl = len(self.layers)
        self.colc = P.sbuf("colc", [128, nl, 32], F32); self.COLC = LT("colc")
        self.rowc = P.sbuf("rowc", [128, RCW], F32); self.ROWC = LT("rowc")
        self.cm = P.sbuf("cm", [128, 8, 2], F32); self.CM = LT("cm")
        self.modc = P.sbuf("modc", [128, 24, 2], F32); self.MODC = LT("modc")
        self.acol = P.sbuf("acol", [128, 8, 2], F32); self.ACOL = LT("acol")
        self.lyr = P.sbuf("lyr", [128, 16], F32); self.LYR = LT("lyr")
        self.gains = P.sbuf("gains", [128, 64], F32); self.GAINS = LT("gains")
        self.gqc = P.sbuf("gqc", [128, 64], F32); self.GQC = LT("gqc")
        self.s_ps = [P.psum("s_ps%d" % i, [128, 512], F32) for i in range(3)]
        self.S = [LT("s_ps%d" % i) for i in range(3)]
        self.s_i = 0
        self.o_ps = [P.psum("o_ps%d" % i, [128, 512], F32) for i in range(2)]
        self.O = [LT("o_ps%d" % i) for i in range(2)]
        self.o_i = 0
        self.m_ps = [P.psum("m_ps%d" % i, [128, 512], F32) for i in range(2)]
        self.M = [LT("m_ps%d" % i) for i in range(2)]
        self.m_i = 0
        self.t_ps = P.psum("t_ps", [128, 1024], BF16)
        self.TPS = LT("t_ps")
        self.OUT = LT("out")
        self.pairs = [[(self.m_ps[0], self.M[0]), (self.m_ps[1], self.M[1])], [(self.s_ps[0], self.S[0]), (self.s_ps[1], self.S[1])]]
        self.pair_i = 0
        self.inter_gen = None

    def prologue(self):
        P = self.P
        for tt in range(NT):
            src = self.x_d[tt * 128:(tt + 1) * 128, :] if tt < NLT else self.ctx_d[(tt - NLT) * 128:(tt - NLT + 1) * 128, :]
            P.dma_in(self.X[tt], self.x_sb[:, tt, :], src)
        P.dma_in(self.IDENT, self.ident[:], self.ident_d)
        P.dma_in(self.HMASK, self.hmask[:], self.hmask_d)
        for i in range(2):
            P.op('pool', lambda e, i=i: e.memset(self.qTs[i][:], 0.0), writes=(self.QTs[i],))
        P.op('pool', lambda e: e.memset(self.kt[:], 0.0), writes=tuple(self.KT))
        P.dma_in(self.ROPE, self.rope[:], self.rope_d.rearrange("p (t c) -> p t c", c=96))
        P.dma_in(self.COLC, self.colc[:], self.colc_d.rearrange("p (l c) -> p l c", c=32))
        P.dma_in(self.CM, self.cm[:], self.cmat_d.rearrange("p (k c) -> p k c", c=2))
        P.op('dve', lambda e: e.tensor_copy(out=self.identb[:], in_=self.ident[:]), reads=(self.IDENT,), writes=(self.IDENTB,))
        P.op('dve', lambda e: e.memset(self.onesb[:], 1.0), writes=(self.ONESB,))
        vv = self.v[:].rearrange("p t (h c) -> p t h c", c=65)
        P.op('dve', lambda e: e.memset(vv[:, :, :, 64:65], 1.0), writes=tuple(self.V))
        P.op('act', lambda e: e.activation(out=self.small[:, 0:16], in_=self.cm[:].rearrange("p k c -> p (k c)"), func=AF.Tanh, scale=0.5),
             reads=(self.CM,), writes=(self.SMALL,))
        P.op('dve', lambda e: e.tensor_scalar(out=self.small[:, 0:16], in0=self.small[:, 0:16], scalar1=0.5, scalar2=0.5, op0=ALU.mult, op1=ALU.add),
             reads=(self.SMALL,), writes=(self.SMALL,))
        P.op('dve', lambda e: e.tensor_tensor(out=self.cm[:].rearrange("p k c -> p (k c)"), in0=self.cm[:].rearrange("p k c -> p (k c)"), in1=self.small[:, 0:16], op=ALU.mult),
             reads=(self.SMALL, self.CM), writes=(self.CM,))

    def next_m(self):
        i = self.m_i
        self.m_i = (i + 1) % 2
        return self.m_ps[i], self.M[i]

    def load_wunit(self, src_ap, ncols, scale_col=None):
        P = self.P
        ui = self.wu_i; self.wu_i = (ui + 1) % 2
        wu, WU = self.wu[ui], self.WU[ui]
        c = 0
        while c < ncols:
            w = min(128, ncols - c)
            si = self.ws_i; self.ws_i = (si + 1) % 2
            ws, WS = self.ws[si], self.WS[si]
            P.dma_in(WS, ws[:, :, 0:w], src_ap[:, c:c + w].rearrange("(k p) n -> p k n", p=128))
            P.op('pool', lambda e, ws=ws, c=c, w=w: e.tensor_copy(out=wu[:, :, c:c + w], in_=ws[:, :, 0:w]), reads=(WS,), writes=(WU,))
            c += w
        return wu, WU

    def adaln(self, li):
        P = self.P
        mp, MP = self.m_ps[0], self.M[0]
        for j in range(24):
            si = self.ws_i; self.ws_i = (si + 1) % 2
            ws, WS = self.ws[si], self.WS[si]
            P.dma_in(WS, ws[:], self.w_ada_d[li][:, j * 128:(j + 1) * 128].rearrange("(k p) n -> p k n", p=128))
            for k in range(8):
                P.mm(lambda e, k=k, j=j, ws=ws: e.matmul(mp[:, 2 * j:2 * j + 2], lhsT=ws[:, k, :], rhs=self.cm[:, k, :], start=(k == 0), stop=(k == 7)),
                     reads=(WS, self.CM), writes=(MP,), signal=(k == 7))
        colc = self.colc
        P.op('dve', lambda e: e.tensor_tensor(out=self.modc[:], in0=mp[:, 0:48].rearrange("p (j c) -> p j c", c=2),
                                              in1=colc[:, li, 8:32].unsqueeze(2).to_broadcast([128, 24, 2]), op=ALU.add),
             reads=(MP, self.COLC), writes=(self.MODC,))
        P.op('dve', lambda e: e.tensor_scalar(out=self.acol[:], in0=self.modc[:, 8:16, :], scalar1=1.0, scalar2=None, op0=ALU.add),
             reads=(self.MODC,), writes=(self.ACOL,))
        P.op('dve', lambda e: e.tensor_tensor(out=self.acol[:], in0=self.acol[:], in1=colc[:, li, 0:8].unsqueeze(2).to_broadcast([128, 8, 2]), op=ALU.mult),
             reads=(self.ACOL, self.COLC), writes=(self.ACOL,))
        for vv in range(2):
            gcol = self.modc[:, 16:24, vv]
            P.op('dve', lambda e, vv=vv, gcol=gcol: e.tensor_copy(out=self.ghl[:, vv, 0, :], in_=gcol), reads=(self.MODC,), writes=(self.GHL,))
            P.op('dve', lambda e, vv=vv, gcol=gcol: e.tensor_tensor(out=self.gtmp[:, vv * 8:(vv + 1) * 8], in0=gcol, in1=self.ghl[:, vv, 0, :], op=ALU.subtract),
                 reads=(self.MODC, self.GHL), writes=(self.GTMP,))
            P.op('dve', lambda e, vv=vv: e.tensor_copy(out=self.ghl[:, vv, 1, :], in_=self.gtmp[:, vv * 8:(vv + 1) * 8]), reads=(self.GTMP, self.GHL), writes=(self.GHL,))
        if 'mod' in self.dbg and li == 0:
            P.dma_out(self.MODC, self.dbg_d['mod'], self.modc[:].rearrange("p j c -> p (j c)"), self.OUT)

    def layer_consts(self, li, l_global):
        P = self.P
        rc = self.rowc
        lam_init = 0.8 - 0.6 * math.exp(-0.3 * l_global)
        sm = self.small
        P.op('dve', lambda e: e.tensor_tensor(out=sm[:, 0:32], in0=rc[:, RC['lq1']:RC['lq1'] + 32], in1=rc[:, RC['lk1']:RC['lk1'] + 32], op=ALU.mult),
             reads=(self.ROWC,), writes=(self.SMALL,))
        P.op('dve', lambda e: e.tensor_tensor(out=sm[:, 32:64], in0=rc[:, RC['lq2']:RC['lq2'] + 32], in1=rc[:, RC['lk2']:RC['lk2'] + 32], op=ALU.mult),
             reads=(self.ROWC, self.SMALL), writes=(self.SMALL,))
        P.op('dve', lambda e: e.tensor_reduce(out=sm[:, 64:66], in_=sm[:, 0:64].rearrange("p (a b) -> p a b", b=32), op=ALU.add, axis=AX.X),
             reads=(self.SMALL,), writes=(self.SMALL,))
        P.op('act', lambda e: e.activation(out=sm[:, 66:68], in_=sm[:, 64:66], func=AF.Exp), reads=(self.SMALL,), writes=(self.SMALL,))
        P.op('dve', lambda e: e.scalar_tensor_tensor(out=self.lyr[:, 0:1], in0=sm[:, 67:68], scalar=-lam_init, op0=ALU.add, in1=sm[:, 66:67], op1=ALU.subtract),
             reads=(self.SMALL,), writes=(self.LYR,))
        for gi, (qn, kn, hd) in enumerate((('dqn', 'dkn', 32), ('gqn', 'gkn', 64), ('nqn', 'nkn', 64))):
            P.op('dve', lambda e, qn=qn, hd=hd: e.tensor_reduce(out=sm[:, 70:71], in_=rc[:, RC[qn]:RC[qn] + hd], op=ALU.max, axis=AX.X, apply_absolute_value=True),
                 reads=(self.ROWC, self.SMALL), writes=(self.SMALL,))
            P.op('dve', lambda e, kn=kn, hd=hd: e.tensor_reduce(out=sm[:, 71:72], in_=rc[:, RC[kn]:RC[kn] + hd], op=ALU.max, axis=AX.X, apply_absolute_value=True),
                 reads=(self.ROWC, self.SMALL), writes=(self.SMALL,))
            P.op('dve', lambda e, gi=gi, hd=hd: e.scalar_tensor_tensor(out=self.lyr[:, 1 + gi:2 + gi], in0=sm[:, 70:71], scalar=-math.sqrt(hd), op0=ALU.mult, in1=sm[:, 71:72], op1=ALU.mult),
                 reads=(self.SMALL, self.LYR), writes=(self.LYR,))
        P.op('dve', lambda e: e.tensor_scalar(out=self.gains[:], in0=rc[:, RC['subln']:RC['subln'] + 64], scalar1=(1.0 - lam_init), scalar2=None, op0=ALU.mult),
             reads=(self.ROWC,), writes=(self.GAINS,))
        P.op('dve', lambda e: e.tensor_scalar(out=self.gqc[:], in0=rc[:, RC['nqn']:RC['nqn'] + 64], scalar1=0.125, scalar2=None, op0=ALU.mult),
             reads=(self.ROWC,), writes=(self.GQC,))

    def build_hT(self, li):
        P = self.P
        sm = self.small
        for tt in range(NT):
            P.op('act', lambda e, tt=tt: e.activation(out=self.sa[:], in_=self.x_sb[:, tt, :], func=AF.Square, accum_out=sm[:, 100 + tt:101 + tt]),
                 reads=(self.X[tt], self.SMALL), writes=(self.SA, self.SMALL))
        P.op('act', lambda e: e.activation(out=sm[:, 120:120 + NT], in_=sm[:, 100:100 + NT], func=AF.Ln, scale=1.0 / D, bias=EPS),
             reads=(self.SMALL,), writes=(self.SMALL,))
        P.op('act', lambda e: e.activation(out=sm[:, 140:140 + NT], in_=sm[:, 120:120 + NT], func=AF.Exp, scale=-0.5),
             reads=(self.SMALL,), writes=(self.SMALL,))
        xh = [(self.sa, self.SA), (self.qf[:].rearrange("p a b -> p (a b)"), self.QF), (self.sq[:].rearrange("p a b -> p (a b)"), self.SQ)]
        for tiles in ([0, 1, 2], [3, 4, 5], [6, 7, 8], [9, 10, 11], [12, 13, 14], [15], [16, 17]):
            t0 = tiles[0]
            nt = len(tiles)
            vv = 0 if t0 < NLT else 1
            for i, tt in enumerate(tiles):
                xs, XS = xh[i]
                P.op('dve', lambda e, tt=tt, xs=xs: e.tensor_scalar(out=xs[:, 0:1024], in0=self.x_sb[:, tt, :], scalar1=sm[:, 140 + tt:141 + tt], scalar2=None, op0=ALU.mult),
                     reads=(self.X[tt], self.SMALL), writes=(XS,))
            for k in range(8):
                mp, MP = self.next_m()
                for i, tt in enumerate(tiles):
                    xs, XS = xh[i]
                    P.mm(lambda e, k=k, i=i, mp=mp, xs=xs: e.transpose(out=mp[:, i * 128:(i + 1) * 128], in_=xs[:, k * 128:(k + 1) * 128], identity=self.ident[:]),
                         reads=(XS, self.IDENT), writes=(MP,), signal=(i == nt - 1))
                P.op('act', lambda e, k=k, mp=mp, t0=t0, vv=vv, nt=nt: e.activation(out=self.hT[:, k, t0 * 128:(t0 + nt) * 128], in_=mp[:, 0:nt * 128], func=AF.Identity,
                                                                           scale=self.acol[:, k, vv:vv + 1], bias=self.modc[:, k, vv:vv + 1]),
                     reads=(MP, self.ACOL, self.MODC), writes=tuple(self.HT[tt] for tt in tiles))
        if 'hT' in self.dbg and li == 0:
            P.dma_out_multi(self.HT, self.dbg_d['hT'], self.hT[:].rearrange("p k t -> p (k t)"), self.OUT)

    def project(self, wu, WU, ncols, tt):
        P = self.P
        mp, MP = self.next_m()
        for k in range(8):
            P.mm(lambda e, k=k: e.matmul(mp[:, 0:ncols], lhsT=self.hT[:, k, tt * 128:(tt + 1) * 128], rhs=wu[:, k, 0:ncols], start=(k == 0), stop=(k == 7)),
                 reads=(self.HT[tt], WU), writes=(MP,), signal=(k == 7))
        return mp, MP

    def norm_rope(self, src, SRC, ncols, hd, gain_ap, rope_kind, tt, out_ap, OUTT, GAINT=None):
        P = self.P
        nh = ncols // hd
        half = hd // 2
        sm = self.small
        qf3 = self.qf[:, 0:ncols].rearrange("p (h d) -> p h d", d=hd)
        P.op('act', lambda e: e.activation(out=self.qf[:, 0:ncols], in_=src, func=AF.Copy), reads=(SRC,), writes=(self.QF,))
        P.op('act', lambda e: e.activation(out=self.sq[:, 0:ncols], in_=src, func=AF.Square), reads=(SRC,), writes=(self.SQ,))
        P.op('dve', lambda e: e.tensor_reduce(out=sm[:, 200:200 + nh], in_=self.sq[:, 0:ncols].rearrange("p (h d) -> p h d", d=hd), op=ALU.add, axis=AX.X),
             reads=(self.SQ, self.SMALL), writes=(self.SMALL,))
        P.op('act', lambda e: e.activation(out=sm[:, 220:220 + nh], in_=sm[:, 200:200 + nh], func=AF.Ln, scale=1.0 / hd, bias=EPS),
             reads=(self.SMALL,), writes=(self.SMALL,))
        P.op('act', lambda e: e.activation(out=sm[:, 240:240 + nh], in_=sm[:, 220:220 + nh], func=AF.Exp, scale=-0.5),
             reads=(self.SMALL,), writes=(self.SMALL,))
        P.op('dve', lambda e: e.tensor_tensor(out=qf3, in0=qf3, in1=sm[:, 240:240 + nh].unsqueeze(2).to_broadcast([128, nh, hd]), op=ALU.mult),
             reads=(self.QF, self.SMALL), writes=(self.QF,))
        gain_b = gain_ap.unsqueeze(1).to_broadcast([128, nh, hd])
        GAINT = GAINT or self.ROWC
        if rope_kind is None or tt >= NLT:
            P.op('dve', lambda e: e.tensor_tensor(out=out_ap.rearrange("p (h d) -> p h d", d=hd), in0=qf3, in1=gain_b, op=ALU.mult),
                 reads=(self.QF, GAINT), writes=(OUTT,))
            return
        P.op('dve', lambda e: e.tensor_tensor(out=qf3, in0=qf3, in1=gain_b, op=ALU.mult), reads=(self.QF, GAINT), writes=(self.QF,))
        c0 = 0 if rope_kind == 'B' else 64
        cos = self.rope[:, tt, c0:c0 + half]
        sin = self.rope[:, tt, c0 + half:c0 + 2 * half]
        q4 = self.qf[:, 0:ncols].rearrange("p (h two f) -> p h two f", two=2, f=half)
        tmp = self.sq[:, 0:ncols].rearrange("p (h two f) -> p h two f", two=2, f=half)
        o4 = out_ap.rearrange("p (h two f) -> p h two f", two=2, f=half)
        cosb = cos.unsqueeze(1).to_broadcast([128, nh, half])
        sinb = sin.unsqueeze(1).to_broadcast([128, nh, half])
        P.op('pool', lambda e: e.tensor_tensor(out=tmp[:, :, 0, :], in0=q4[:, :, 1, :], in1=sinb, op=ALU.mult), reads=(self.QF, self.ROPE), writes=(self.SQ,))
        P.op('dve', lambda e: e.tensor_tensor(out=tmp[:, :, 1, :], in0=q4[:, :, 0, :], in1=cosb, op=ALU.mult), reads=(self.QF, self.ROPE, self.SQ), writes=(self.SQ,))
        P.op('dve', lambda e: e.tensor_tensor(out=o4[:, :, 0, :], in0=tmp[:, :, 1, :], in1=tmp[:, :, 0, :], op=ALU.subtract), reads=(self.SQ,), writes=(OUTT,))
        P.op('pool', lambda e: e.tensor_tensor(out=tmp[:, :, 0, :], in0=q4[:, :, 0, :], in1=sinb, op=ALU.mult), reads=(self.QF, self.ROPE, self.SQ), writes=(self.SQ,))
        P.op('dve', lambda e: e.tensor_tensor(out=tmp[:, :, 1, :], in0=q4[:, :, 1, :], in1=cosb, op=ALU.mult), reads=(self.QF, self.ROPE, self.SQ), writes=(self.SQ,))
        P.op('dve', lambda e: e.tensor_tensor(out=o4[:, :, 1, :], in0=tmp[:, :, 1, :], in1=tmp[:, :, 0, :], op=ALU.add), reads=(self.SQ, OUTT), writes=(OUTT,))

    def transpose_to(self, src, SRC, ncols, dst_fn, DST):
        P = self.P
        nch = (ncols + 127) // 128
        for c in range(nch):
            w = min(128, ncols - c * 128)
            P.mm(lambda e, c=c, w=w: e.transpose(out=self.t_ps[0:w, c * 128:(c + 1) * 128], in_=src[:, c * 128:c * 128 + w], identity=self.identb[:]),
                 reads=(SRC, self.IDENTB), writes=(self.TPS,), signal=(c == nch - 1))
        for c in range(nch):
            w = min(128, ncols - c * 128)
            P.op('dve', lambda e, c=c, w=w: e.tensor_copy(out=dst_fn(c, w), in_=self.t_ps[0:w, c * 128:(c + 1) * 128]), reads=(self.TPS,), writes=(DST,))

    def proj_batch(self, wu, WU, n, tiles, force_pair=None):
        P = self.P
        if force_pair is None:
            pair = self.pairs[self.pair_i]; self.pair_i ^= 1
        else:
            pair = self.pairs[force_pair]
        for i, tt in enumerate(tiles):
            ps, PS = pair[i // 2]
            off = (i % 2) * 256
            for k in range(8):
                P.mm(lambda e, k=k, ps=ps, off=off, tt=tt: e.matmul(ps[:, off:off + n], lhsT=self.hT[:, k, tt * 128:(tt + 1) * 128], rhs=wu[:, k, 0:n], start=(k == 0), stop=(k == 7)),
                     reads=(self.HT[tt], WU), writes=(PS,), signal=(k == 7))
        return pair

    def bank_views(self, pair, nt, n):
        out = []
        for b in range((nt + 1) // 2):
            ntb = min(2, nt - 2 * b)
            ps, PS = pair[b]
            out.append((b, ntb, ps[:, 0:512].rearrange("p (t c) -> p t c", c=256)[:, 0:ntb, 0:n], PS))
        return out

    def norm_rope_batch(self, pair, n, hd, gain_ap, GAINT, rope_kind, tiles):
        P = self.P
        nt = len(tiles)
        nh = n // hd
        half = hd // 2
        sm = self.small
        for (b, ntb, src, PS) in self.bank_views(pair, nt, n):
            P.op('act', lambda e, b=b, ntb=ntb, src=src: e.activation(out=self.sq[:, 2 * b:2 * b + ntb, 0:n], in_=src, func=AF.Square), reads=(PS,), writes=(self.SQ,))
            gb = gain_ap.unsqueeze(1).to_broadcast([128, nh, hd])
            for t in range(ntb):
                P.op('dve', lambda e, b=b, t=t, src=src, gb=gb: e.tensor_tensor(out=self.qf[:, 2 * b + t, 0:n].rearrange("p (h d) -> p h d", d=hd),
                                                                             in0=src[:, t, :].rearrange("p (h d) -> p h d", d=hd), in1=gb, op=ALU.mult),
                     reads=(PS, GAINT, self.SQ), writes=(self.QF,))
        m = nt * nh
        q4 = self.qf[:, 0:nt, 0:n].rearrange("p t (h d) -> p t h d", d=hd)
        ss = sm[:, 200:200 + m].rearrange("p (t h) -> p t h", h=nh)
        P.op('dve', lambda e: e.tensor_reduce(out=ss, in_=self.sq[:, 0:nt, 0:n].rearrange("p t (h d) -> p t h d", d=hd), op=ALU.add, axis=AX.X),
             reads=(self.SQ, self.SMALL), writes=(self.SMALL,))
        P.op('act', lambda e: e.activation(out=sm[:, 240:240 + m], in_=sm[:, 200:200 + m], func=AF.Ln, scale=1.0 / hd, bias=EPS), reads=(self.SMALL,), writes=(self.SMALL,))
        P.op('act', lambda e: e.activation(out=sm[:, 280:280 + m], in_=sm[:, 240:240 + m], func=AF.Exp, scale=-0.5), reads=(self.SMALL,), writes=(self.SMALL,))
        rstd_b = sm[:, 280:280 + m].rearrange("p (t h) -> p t h", h=nh).unsqueeze(3).to_broadcast([128, nt, nh, hd])
        gain_b = gain_ap.unsqueeze(1).unsqueeze(1).to_broadcast([128, nt, nh, hd])
        o4 = self.qb[:, 0:nt, 0:n].rearrange("p t (h d) -> p t h d", d=hd)
        if rope_kind is None or tiles[0] >= NLT:
            P.op('dve', lambda e: e.tensor_tensor(out=o4, in0=q4, in1=rstd_b, op=ALU.mult), reads=(self.QF, self.SMALL), writes=(self.QB,))
            return
        P.op('dve', lambda e: e.tensor_tensor(out=q4, in0=q4, in1=rstd_b, op=ALU.mult), reads=(self.QF, self.SMALL), writes=(self.QF,))
        c0 = 0 if rope_kind == 'B' else 64
        tt0 = tiles[0]
        cosb = self.rope[:, tt0:tt0 + nt, c0:c0 + half].unsqueeze(2).to_broadcast([128, nt, nh, half])
        sinb = self.rope[:, tt0:tt0 + nt, c0 + half:c0 + 2 * half].unsqueeze(2).to_broadcast([128, nt, nh, half])
        q5 = self.qf[:, 0:nt, 0:n].rearrange("p t (h two f) -> p t h two f", two=2, f=half)
        t5 = self.sq[:, 0:nt, 0:n].rearrange("p t (h two f) -> p t h two f", two=2, f=half)
        o5 = self.qb[:, 0:nt, 0:n].rearrange("p t (h two f) -> p t h two f", two=2, f=half)
        x1, x2 = q5[:, :, :, 0, :], q5[:, :, :, 1, :]
        ta, tb = t5[:, :, :, 0, :], t5[:, :, :, 1, :]
        P.op('dve', lambda e: e.tensor_tensor(out=ta, in0=x2, in1=sinb, op=ALU.mult), reads=(self.QF, self.ROPE), writes=(self.SQ,))
        P.op('dve', lambda e: e.tensor_tensor(out=tb, in0=x1, in1=cosb, op=ALU.mult), reads=(self.QF, self.ROPE, self.SQ), writes=(self.SQ,))
        P.op('dve', lambda e: e.tensor_tensor(out=o5[:, :, :, 0, :], in0=tb, in1=ta, op=ALU.subtract), reads=(self.SQ,), writes=(self.QB,))
        P.op('dve', lambda e: e.tensor_tensor(out=ta, in0=x1, in1=sinb, op=ALU.mult), reads=(self.QF, self.ROPE, self.SQ), writes=(self.SQ,))
        P.op('dve', lambda e: e.tensor_tensor(out=tb, in0=x2, in1=cosb, op=ALU.mult), reads=(self.QF, self.ROPE, self.SQ), writes=(self.SQ,))
        P.op('dve', lambda e: e.tensor_tensor(out=o5[:, :, :, 1, :], in0=tb, in1=ta, op=ALU.add), reads=(self.SQ, self.QB), writes=(self.QB,))

    def transpose_batch(self, n, nt, dst_fn, DSTS, extra_swap=None):
        P = self.P
        nch = (n + 127) // 128
        last = (nch - 1, nt - 1)
        for c in range(nch):
            w = min(128, n - c * 128)
            for i in range(nt):
                blk = c * nt + i
                P.mm(lambda e, c=c, w=w, i=i, blk=blk: e.transpose(out=self.t_ps[0:w, blk * 128:(blk + 1) * 128], in_=self.qb[:, i, c * 128:c * 128 + w], identity=self.identb[:]),
                     reads=(self.QB, self.IDENTB), writes=(self.TPS,), signal=((c, i) == last and extra_swap is None))
        if extra_swap is not None:
            for i in range(nt):
                blk = nt + i
                P.mm(lambda e, i=i, blk=blk: e.transpose(out=self.t_ps[0:64, blk * 128:(blk + 1) * 128], in_=self.qb[:, i, 64:128], identity=self.identb[:]),
                     reads=(self.QB, self.IDENTB), writes=(self.TPS,), signal=False)
                P.mm(lambda e, i=i, blk=blk: e.transpose(out=self.t_ps[64:128, blk * 128:(blk + 1) * 128], in_=self.qb[:, i, 0:64], identity=self.identb[:]),
                     reads=(self.QB, self.IDENTB), writes=(self.TPS,), signal=(i == nt - 1))
        for c in range(nch):
            w = min(128, n - c * 128)
            dst = dst_fn(c, w)
            P.op('dve', lambda e, c=c, w=w, dst=dst: e.tensor_copy(out=dst, in_=self.t_ps[0:w, c * nt * 128:(c + 1) * nt * 128]), reads=(self.TPS,), writes=tuple(DSTS))
        if extra_swap is not None:
            P.op('dve', lambda e: e.tensor_copy(out=extra_swap, in_=self.t_ps[:, nt * 128:2 * nt * 128]), reads=(self.TPS,) + tuple(DSTS), writes=tuple(DSTS))

    def kv_phase(self, li, g):
        P = self.P
        rc = self.rowc
        hd = g['hd']
        k0, kc = g['k']
        v0, vc = g['v']
        nv = g['nv']
        wk, WK = self.load_wunit(self.w_in_d[li][:, k0:k0 + kc], kc)
        wv, WV = self.load_wunit(self.w_in_d[li][:, v0:v0 + vc], vc)
        gain = rc[:, RC[g['kg']]:RC[g['kg']] + hd]
        batches = [[0, 1, 2, 3], [4, 5, 6, 7], [8, 9, 10, 11], [12, 13, 14, 15], [16, 17]]
        proj = {}

        def issue_proj(bi):
            proj[bi] = (self.proj_batch(wk, WK, kc, batches[bi]), self.proj_batch(wv, WV, vc, batches[bi]))
        issue_proj(0)
        for bi, tiles in enumerate(batches):
            nt = len(tiles)
            tt0 = tiles[0]
            pair, pairv = proj[bi]
            self.norm_rope_batch(pair, kc, hd, gain, self.ROWC, g['rope'], tiles)
            for (b, ntb, src, PS) in self.bank_views(pairv, nt, vc):
                t0 = tt0 + 2 * b
                vdst = self.v[:, t0:t0 + ntb, 0:nv * 65].rearrange("p t (h c) -> p t h c", c=65)[:, :, :, 0:64]
                P.op('act', lambda e, src=src, vdst=vdst: e.activation(out=vdst, in_=src.rearrange("p t (h c) -> p t h c", c=64), func=AF.Copy),
                     reads=(PS,), writes=tuple(self.V[t0 + j] for j in range(ntb)))
            if bi + 1 < len(batches):
                issue_proj(bi + 1)
            dsts = [self.KT[tt] for tt in tiles]
            if g['kind'] == 'B':
                self.transpose_batch(kc, nt, lambda c, w, tt0=tt0, nt=nt: self.kt[0:w, 0, tt0 * 128:(tt0 + nt) * 128], dsts,
                                     extra_swap=self.kt[:, 1, tt0 * 128:(tt0 + nt) * 128])
            else:
                self.transpose_batch(kc, nt, lambda c, w, tt0=tt0, nt=nt: self.kt[0:w, c, tt0 * 128:(tt0 + nt) * 128], dsts)
        if 'kt' in self.dbg and li == 0 and g['name'] == self.dbg_group:
            P.dma_out_multi(self.KT, self.dbg_d['kt'], self.kt[:].rearrange("p k t -> p (k t)"), self.OUT)
            P.dma_out_multi(self.V, self.dbg_d['v'], self.v[:].rearrange("p t c -> p (t c)"), self.OUT)

    def load_q_unit(self, li, g, c0, n):
        P = self.P
        if g['kind'] != 'B':
            return self.load_wunit(self.w_in_d[li][:, g['q'][0] + c0:g['q'][0] + c0 + n], n)
        si = self.ws_i; self.ws_i = (si + 1) % 2
        ui = self.wu_i; self.wu_i = (ui + 1) % 2
        ws, WS, wu, WU = self.ws[si], self.WS[si], self.wu[ui], self.WU[ui]
        q0, qc = g['q']
        j = c0 // 128
        for kv in range(2):
            hsrc = q0 + (3 * kv + j) * 64
            P.dma_in(WS, ws[:, :, kv * 64:(kv + 1) * 64], self.w_in_d[li][:, hsrc:hsrc + 64].rearrange("(k p) n -> p k n", p=128))
        P.op('pool', lambda e: e.tensor_copy(out=wu[:, :, 0:n], in_=ws[:, :, 0:n]), reads=(WS,), writes=(WU,))
        return wu, WU

    def col_units(self, total):
        out = []
        c = 0
        while c < total:
            n = min(256, total - c)
            out.append((c, n))
            c += n
        return out

    def load_wo(self, li, g, vv):
        P = self.P
        for half in range(2):
            mp2, MP2 = self.m_ps[half], self.M[half]
            for kk in range(4):
                k = half * 4 + kk
                for hl in range(2):
                    P.op('dve', lambda e, k=k, hl=hl: e.tensor_scalar(out=self.dhl[:, hl, :], in0=self.identb[:], scalar1=self.ghl[:, vv, hl, k:k + 1], scalar2=None, op0=ALU.mult),
                         reads=(self.IDENTB, self.GHL), writes=(self.DHL,))
                for hl in range(2):
                    P.mm(lambda e, kk=kk, mp2=mp2, hl=hl: e.matmul(mp2[:, kk * 128:(kk + 1) * 128], lhsT=self.onesb[:], rhs=self.dhl[:, hl, :],
                                                                   start=(kk == 0 and hl == 0), stop=(kk == 3 and hl == 1)),
                         reads=(self.ONESB, self.DHL), writes=(MP2,), signal=(hl == 1))
        r0, rows = g['wo']
        nch = (rows + 127) // 128
        for c in range(nch):
            w = min(128, rows - c * 128)
            si = self.ws_i; self.ws_i = (si + 1) % 2
            ws, WS = self.ws[si], self.WS[si]
            wsf = ws[:].rearrange("p k n -> p (k n)")
            P.dma_in(WS, wsf[0:w, 0:D], self.w_out_d[li][r0 + c * 128:r0 + c * 128 + w, :])
            for half in range(2):
                P.op('dve', lambda e, c=c, w=w, half=half, wsf=wsf: e.tensor_tensor(out=self.wo[0:w, c, half * 512:(half + 1) * 512], in0=wsf[0:w, half * 512:(half + 1) * 512],
                                                                                   in1=self.m_ps[half][0:w, :], op=ALU.mult),
                     reads=(WS, self.M[half]), writes=(self.WO,))

    def load_bias_part(self, li, h, col0, ncols, dst, DST):
        P = self.P
        c = 0
        while c < ncols:
            w = min(1024, ncols - c)
            si = self.ws_i; self.ws_i = (si + 1) % 2
            ws, WS = self.ws[si], self.WS[si]
            wsf = ws[:].rearrange("p k n -> p (k n)")
            P.dma_in(WS, wsf[:, 0:w], self.bias_d[li, h][:, col0 + c:col0 + c + w])
            P.op('pool', lambda e, c=c, w=w, wsf=wsf: e.tensor_copy(out=dst[:, c:c + w], in_=wsf[:, 0:w]), reads=(WS,), writes=(DST,))
            c += w

    def attention(self, jobs):
        P = self.P
        flat = []
        for job in jobs:
            nb = len(job['blocks'])
            oi = self.o_i; self.o_i = (oi + 1) % 2
            job['o'] = (self.o_ps[oi], self.O[oi])
            for bi, b in enumerate(job['blocks']):
                flat.append((job, bi, nb, b))
        n = len(flat)
        st = [None] * n
        self.inter_every = max(1, min(3, n // 40))

        def prep_mask(job):
            if 'qm' in job:
                return
            mi = self.qm_i; self.qm_i = (mi + 1) % 3
            job['qm'] = (self.qm[mi], self.QM[mi])
            qm_, QM_ = job['qm']
            mc = job['mcol']
            nq_ = job['blocks'][0]['nq']
            P.op('pool', lambda e: e.tensor_scalar(out=qm_[:, 0:nq_], in0=job['qsrc'], scalar1=self.hmask[:, mc:mc + 1], scalar2=1.0, op0=ALU.mult, op1=ALU.mult),
                 reads=(self.QT, self.HMASK), writes=(QM_,))

        def emit_qk(i):
            job, bi, nb, b = flat[i]
            si = self.s_i; self.s_i = (si + 1) % 3
            sp, SP_ = self.s_ps[si], self.S[si]
            nq = b['nq']
            bias = b.get('bias') or []
            if bi == 0:
                prep_mask(job)
                ji = jobs.index(job)
                if ji + 1 < len(jobs):
                    prep_mask(jobs[ji + 1])
            qm_, QM_ = job['qm']
            P.mm(lambda e: e.matmul(sp[:, 0:nq], lhsT=b['kt'], rhs=qm_[:, 0:nq], start=True, stop=(len(bias) == 0)),
                 reads=(b['KT'], QM_), writes=(SP_,), signal=(len(bias) == 0))
            for xi, (c0, c1, uap) in enumerate(bias):
                last = xi == len(bias) - 1
                P.mm(lambda e, c0=c0, c1=c1, uap=uap, last=last: e.matmul(sp[:, c0:c1], lhsT=self.identb[:], rhs=uap, start=False, stop=last),
                     reads=(self.IDENTB,) + tuple(b['ULT']), writes=(SP_,), signal=last)
            pi = self.pt_i; self.pt_i = (pi + 1) % 3
            pt, PT_ = self.pt[pi], self.PT[pi]
            P.op('act', lambda e: e.activation(out=pt[:, 0:nq], in_=sp[:, 0:nq], func=AF.Exp, scale=job['scale'], bias=SOFTMAX_SHIFT),
                 reads=(SP_,), writes=(PT_,))
            st[i] = (pt, PT_)

        def emit_pv(i):
            job, bi, nb, b = flat[i]
            pt, PT_ = st[i]
            op_, O_ = job['o']
            nsub = b['nq'] // 128
            for j in range(nsub):
                P.mm(lambda e, j=j: e.matmul(op_[:, j * 65:(j + 1) * 65], lhsT=pt[:, j * 128:(j + 1) * 128], rhs=b['v'],
                                             start=(bi == 0 and j == 0), stop=(bi == nb - 1 and j == nsub - 1)),
                     reads=(PT_, b['V']), writes=(O_,), signal=(j == nsub - 1))
            if bi == nb - 1:
                job['finish'](op_, O_)

        for i in range(min(2, n)):
            emit_qk(i)
        for i in range(n):
            if i + 2 < n:
                emit_qk(i + 2)
            emit_pv(i)
            if self.inter_gen is not None and i % self.inter_every == self.inter_every - 1:
                try:
                    next(self.inter_gen)
                except StopIteration:
                    self.inter_gen = None

    def q_proj_gen(self, li, g, ch, qT, QT, force_pair=None):
        rc = self.rowc
        hd = g['hd']
        kind = g['kind']
        q0, qcols = g['q']
        nsub = 4 if ch < 4 else 2
        nq = nsub * 128
        tts = [ch * 4 + i for i in range(4)] if ch < 4 else [16, 17]
        gain = rc[:, RC[g['qg']]:RC[g['qg']] + hd] if kind != 'C' else self.gqc[:]
        GAINT = self.ROWC if kind != 'C' else self.GQC
        for (c0, n) in (self.col_units(qcols) if kind != 'B' else [(0, 128), (128, 128), (256, 128)]):
            wq, WQ = self.load_q_unit(li, g, c0, n)
            yield
            pair = self.proj_batch(wq, WQ, n, tts, force_pair=force_pair)
            yield
            self.norm_rope_batch(pair, n, hd, gain, GAINT, g['rope'], tts)
            yield
            cbase = c0 // 128
            self.transpose_batch(n, nsub, lambda c, w, cbase=cbase: qT[0:w, cbase + c, 0:nq], [QT])
            yield

    def g_proj_gen(self, li, g, ch):
        P = self.P
        g0, gcols = g['g']
        nsub = 4 if ch < 4 else 2
        tts = [ch * 4 + i for i in range(4)] if ch < 4 else [16, 17]
        for (c0, n) in self.col_units(gcols):
            wg, WG = self.load_wunit(self.w_in_d[li][:, g0 + c0:g0 + c0 + n], n)
            yield
            pair = self.proj_batch(wg, WG, n, tts, force_pair=0)
            yield
            for (b, ntb, src, PS) in self.bank_views(pair, nsub, n):
                P.op('act', lambda e, b=b, ntb=ntb, src=src, n=n: e.activation(out=self.qf[:, 2 * b:2 * b + ntb, 0:n], in_=src, func=AF.Tanh, scale=0.5), reads=(PS,), writes=(self.QF,))
                P.op('dve', lambda e, b=b, ntb=ntb, src=src, c0=c0, n=n: e.scalar_tensor_tensor(out=self.g_sb[:, 2 * b:2 * b + ntb, c0:c0 + n], in0=self.qf[:, 2 * b:2 * b + ntb, 0:n], scalar=1.0, op0=ALU.add, in1=src, op1=ALU.mult),
                     reads=(self.QF, PS), writes=(self.G,))
            yield

    def qg_chunk(self, li, g, ch, is_last_layer, qbuf, next_gen):
        P = self.P
        self.qT, self.QT = qbuf
        self.inter_gen = next_gen
        rc = self.rowc
        hd = g['hd']
        kind = g['kind']
        q0, qcols = g['q']
        g0, gcols = g['g']
        nsub = 4 if ch < 4 else 2
        nq = nsub * 128
        tts = [ch * 4 + i for i in range(4)] if ch < 4 else [16, 17]
        gain = rc[:, RC[g['qg']]:RC[g['qg']] + hd] if kind != 'C' else self.gqc[:]
        GAINT = self.ROWC if kind != 'C' else self.GQC
        if getattr(self, 'stop_after', None) == 'gproj':
            return
        jobs = []
        kblocks_all = list(range(NT)) if ch < 4 else [16, 17]
        if kind == 'A':
            scale = 32 ** -0.5
            shift = self.lyr[:, 1:2]
            for s in range(8):
                h, m = s // 2, s % 2
                chunk, poff = s // 4, 32 * (s % 4)
                blocks = [dict(kt=self.kt[:, chunk, kb * 128:(kb + 1) * 128],
                               v=self.v[:, kb, h * 65:(h + 1) * 65], KT=self.KT[kb], V=self.V[kb], nq=nq) for kb in kblocks_all]
                jq = dict(qsrc=self.qT[:, chunk, 0:nq], mcol=s % 4)

                def fin(op_, O_, h=h, m=m, nsub=nsub):
                    o3 = op_[:, 0:nsub * 65].rearrange("p (j c) -> p j c", c=65)
                    sm = self.small2
                    P.op('dve', lambda e: e.reciprocal(out=sm[:, 0:nsub], in_=o3[:, :, 64]), reads=(O_, self.SMALL2), writes=(self.SMALL2,))
                    P.op('dve', lambda e: e.tensor_tensor(out=self.o01[m][:, 0:nsub, :], in0=o3[:, :, 0:64], in1=sm[:, 0:nsub].unsqueeze(2).to_broadcast([128, nsub, 64]), op=ALU.mult),
                         reads=(O_, self.SMALL2), writes=(self.O01[m],))
                    if m == 1:
                        P.op('dve', lambda e: e.scalar_tensor_tensor(out=self.otok[:, 0:nsub, h * 64:(h + 1) * 64], in0=self.o01[1][:, 0:nsub, :], scalar=self.lyr[:, 0:1], op0=ALU.mult,
                                                                     in1=self.o01[0][:, 0:nsub, :], op1=ALU.add),
                             reads=(self.O01[0], self.O01[1], self.LYR), writes=(self.OTOK,))
                jobs.append(dict(blocks=blocks, scale=scale, shift=shift, finish=fin, **jq))
        else:
            scale = 64 ** -0.5 if kind == 'B' else 1.0
            shift = self.lyr[:, 2:3] if kind == 'B' else self.lyr[:, 3:4]
            for hh in range(g['nh']):
                if kind == 'B':
                    kvh = hh // 3
                    pos = 2 * (hh % 3) + kvh
                    qchunk, poff = pos // 2, 64 * (pos % 2)
                    kchunk = 0 if poff == 64 * kvh else 1
                    vh = kvh
                else:
                    qchunk, poff = hh // 2, 64 * (hh % 2)
                    kchunk = qchunk
                    vh = hh
                if kind == 'C' and ch < 4:
                    blist = c_key_blocks(ch) + [(16, None), (17, None)]
                else:
                    blist = [(kb, None) for kb in kblocks_all]
                blocks = []
                for kb, segs in blist:
                    b = dict(kt=self.kt[:, kchunk, kb * 128:(kb + 1) * 128],
                             v=self.v[:, kb, vh * 65:(vh + 1) * 65], KT=self.KT[kb], V=self.V[kb], nq=nq)
                    if segs:
                        bl = []
                        for (c0, c1, tab, ms) in segs:
                            if tab == 'I':
                                bl.append((c0, c1, self.ubi[:, hh, ms * 64:ms * 64 + (c1 - c0)]))
                            else:
                                bl.append((c0, c1, self.ubfus[hh % 2][:, ms * 64:ms * 64 + (c1 - c0)]))
                        b['bias'] = bl
                        b['ULT'] = (self.UBI[hh], self.UBFUS[hh % 2])
                    blocks.append(b)

                def fin(op_, O_, hh=hh, nsub=nsub):
                    o3 = op_[:, 0:nsub * 65].rearrange("p (j c) -> p j c", c=65)
                    sm = self.small2
                    P.op('dve', lambda e: e.reciprocal(out=sm[:, 0:nsub], in_=o3[:, :, 64]), reads=(O_, self.SMALL2), writes=(self.SMALL2,))
                    P.op('dve', lambda e: e.tensor_tensor(out=self.otok[:, 0:nsub, hh * 64:(hh + 1) * 64], in0=o3[:, :, 0:64], in1=sm[:, 0:nsub].unsqueeze(2).to_broadcast([128, nsub, 64]), op=ALU.mult),
                         reads=(O_, self.SMALL2), writes=(self.OTOK,))
                jobs.append(dict(blocks=blocks, scale=scale, shift=shift, finish=fin, head=hh, qsrc=self.qT[:, qchunk, 0:nq], mcol=4 + poff // 64))
        if kind == 'C' and ch < 4:
            if ch in (0, 3):
                for hh in range(2):
                    self.load_bias_part(li, g['h0'] + hh, UINT_M * 64, UFULL_M * 64, self.ubfus[hh], self.UBFUS[hh])
                fin0 = jobs[0]['finish']

                def fin0_then_load(op_, O_, fin0=fin0):
                    fin0(op_, O_)
                    self.load_bias_part(li, g['h0'] + 2, UINT_M * 64, UFULL_M * 64, self.ubfus[0], self.UBFUS[0])
                jobs[0]['finish'] = fin0_then_load
            self.attention(jobs)
        else:
            self.attention(jobs)
        if 'qt' in self.dbg and li == 0 and g['name'] == self.dbg_group and ch == self.dbg_chunk:
            P.dma_out(self.QT, self.dbg_d['qt'], self.qT[:].rearrange("p c t -> p (c t)"), self.OUT)
            P.dma_out(self.OTOK, self.dbg_d['otok'], self.otok[:].rearrange("p c t -> p (c t)"), self.OUT)
        if self.inter_gen is not None:
            for _ in self.inter_gen:
                pass
            self.inter_gen = None
        if getattr(self, 'stop_after', None) == 'attn':
            return
        if kind == 'A':
            sm = self.small
            m = nsub * 4
            o4 = self.otok[:, 0:nsub, 0:256].rearrange("p t (h d) -> p t h d", d=64)
            P.op('act', lambda e: e.activation(out=self.sq[:, 0:nsub, 0:256], in_=self.otok[:, 0:nsub, 0:256], func=AF.Square), reads=(self.OTOK,), writes=(self.SQ,))
            P.op('dve', lambda e: e.tensor_reduce(out=sm[:, 200:200 + m].rearrange("p (t h) -> p t h", h=4), in_=self.sq[:, 0:nsub, 0:256].rearrange("p t (h d) -> p t h d", d=64), op=ALU.add, axis=AX.X),
                 reads=(self.SQ, self.SMALL), writes=(self.SMALL,))
            P.op('act', lambda e: e.activation(out=sm[:, 240:240 + m], in_=sm[:, 200:200 + m], func=AF.Ln, scale=1.0 / 64, bias=EPS), reads=(self.SMALL,), writes=(self.SMALL,))
            P.op('act', lambda e: e.activation(out=sm[:, 280:280 + m], in_=sm[:, 240:240 + m], func=AF.Exp, scale=-0.5), reads=(self.SMALL,), writes=(self.SMALL,))
            P.op('dve', lambda e: e.tensor_tensor(out=o4, in0=o4, in1=sm[:, 280:280 + m].rearrange("p (t h) -> p t h", h=4).unsqueeze(3).to_broadcast([128, nsub, 4, 64]), op=ALU.mult),
                 reads=(self.OTOK, self.SMALL), writes=(self.OTOK,))
            P.op('dve', lambda e: e.tensor_tensor(out=o4, in0=o4, in1=self.gains[:].unsqueeze(1).unsqueeze(1).to_broadcast([128, nsub, 4, 64]), op=ALU.mult),
                 reads=(self.OTOK, self.GAINS), writes=(self.OTOK,))
        P.op('dve', lambda e: e.scalar_tensor_tensor(out=self.ytok[:, 0:nsub, 0:gcols], in0=self.otok[:, 0:nsub, 0:gcols], scalar=0.5, op0=ALU.mult, in1=self.g_sb[:, 0:nsub, 0:gcols], op1=ALU.mult),
             reads=(self.OTOK, self.G), writes=(self.YTOK,))
        if 'ytok' in self.dbg and li == 0 and g['name'] == self.dbg_group and ch == self.dbg_chunk:
            P.dma_out(self.YTOK, self.dbg_d['ytok'], self.ytok[:].rearrange("p c t -> p (c t)"), self.OUT)
            P.dma_out(self.G, self.dbg_d['g'], self.g_sb[:].rearrange("p c t -> p (c t)"), self.OUT)
        return self.post2_gen(g, ch, (self.qT, self.QT))

    def post2_gen(self, g, ch, ybuf):
        P = self.P
        g0, gcols = g['g']
        nsub = 4 if ch < 4 else 2
        tts = [ch * 4 + i for i in range(4)] if ch < 4 else [16, 17]
        nch = (gcols + 127) // 128
        yT, YT = ybuf
        for i in range(nsub):
            self.transpose_to(self.ytok[:, i, :], self.YTOK, gcols, lambda c, w, i=i, yT=yT: yT[0:w, c, i * 128:(i + 1) * 128], YT)
            yield
        for i, tt in enumerate(tts):
            for half in range(2):
                mp, MP = self.next_m()
                for c in range(nch):
                    w = min(128, gcols - c * 128)
                    P.mm(lambda e, c=c, w=w, i=i, half=half, mp=mp, yT=yT: e.matmul(mp[:], lhsT=yT[0:w, c, i * 128:(i + 1) * 128], rhs=self.wo[0:w, c, half * 512:(half + 1) * 512],
                                                                                  start=(c == 0), stop=(c == nch - 1)),
                         reads=(YT, self.WO), writes=(MP,), signal=(c == nch - 1))
                P.op('dve', lambda e, tt=tt, half=half, mp=mp: e.tensor_tensor(out=self.x_sb[:, tt, half * 512:(half + 1) * 512], in0=mp[:], in1=self.x_sb[:, tt, half * 512:(half + 1) * 512], op=ALU.add),
                     reads=(MP, self.X[tt]), writes=(self.X[tt],))
                yield

    def layer(self, li, is_last):
        l_global = self.layers[li]
        self.P.dma_in(self.ROWC, self.rowc[:], self.rowc_d[:, li * RCW:(li + 1) * RCW])
        sa_ = getattr(self, 'stop_after', None)
        self.adaln(li)
        if sa_ == 'adaln':
            return
        self.layer_consts(li, l_global)
        if sa_ == 'consts':
            return
        self.build_hT(li)
        if sa_ == 'hT':
            return
        groups = GROUPS if not getattr(self, 'only_groups', None) else [g for g in GROUPS if g['name'] in self.only_groups]
        for g in groups:
            self.kv_phase(li, g)
            if sa_ == 'kv':
                return
            chunks = [0, 1, 2, 3] + ([] if is_last else [4])
            if getattr(self, 'only_chunks', None) is not None:
                chunks = [c for c in chunks if c in self.only_chunks]
            if g['kind'] == 'C':
                for hh in range(3):
                    self.load_bias_part(li, g['h0'] + hh, 0, UINT_M * 64, self.ubi[:, hh, :], self.UBI[hh])
            for _ in self.q_proj_gen(li, g, chunks[0], self.qTs[0], self.QTs[0]):
                pass
            pending = None
            for idx, ch in enumerate(chunks):
                if ch == chunks[0] or ch == 4:
                    if pending is not None:
                        for _ in pending:
                            pass
                        pending = None
                    self.load_wo(li, g, 0 if ch < 4 else 1)
                if sa_ == 'wo':
                    return
                gens = [self.g_proj_gen(li, g, ch)]
                if pending is not None:
                    gens.append(pending)
                if idx + 1 < len(chunks):
                    nb = (idx + 1) % 2
                    gens.append(self.q_proj_gen(li, g, chunks[idx + 1], self.qTs[nb], self.QTs[nb], force_pair=0))
                pending = self.qg_chunk(li, g, ch, is_last, (self.qTs[idx % 2], self.QTs[idx % 2]), itertools.chain(*gens))
            if pending is not None:
                for _ in pending:
                    pass

    def epilogue(self):
        P = self.P
        for tt in range(NLT):
            P.dma_out(self.X[tt], self.y_d[tt * 128:(tt + 1) * 128, :], self.x_sb[:, tt, :], self.OUT)
        if self.emit_ctx:
            for tt in range(NLT, NT):
                P.dma_out(self.X[tt], self.ctxo_d[(tt - NLT) * 128:(tt - NLT + 1) * 128, :], self.x_sb[:, tt, :], self.OUT)
        P.wait_dma_done(self.OUT)


def make_builder(layers, last_flags, emit_ctx, dbg=(), **opts):
    b = Builder(layers, last_flags, emit_ctx, dbg)
    for k, v in opts.items():
        setattr(b, k, v)
    return b


def host_consts(inputs, layers):
    nl = len(layers)
    colc = np.zeros((128, nl, 32), np.float32)
    rowc = np.zeros((128, nl, RCW), np.float32)
    for i, l in enumerate(layers):
        colc[:, i, 0:8] = inputs['norm_g'][l].reshape(8, 128).T
        colc[:, i, 8:32] = inputs['b_ada'][l].reshape(24, 128).T
        for nm, key, n in (('dqn', 'diff_q_norm', 32), ('dkn', 'diff_k_norm', 32), ('subln', 'diff_subln', 64),
                           ('gqn', 'gqa_q_norm', 64), ('gkn', 'gqa_k_norm', 64), ('nqn', 'nat_q_norm', 64), ('nkn', 'nat_k_norm', 64),
                           ('lq1', 'lambda_q1', 32), ('lk1', 'lambda_k1', 32), ('lq2', 'lambda_q2', 32), ('lk2', 'lambda_k2', 32)):
            rowc[:, i, RC[nm]:RC[nm] + n] = inputs[key][l][None, :]
    return colc.reshape(128, nl * 32), rowc.reshape(128, nl * RCW)


def core_inputs(inputs, b, layers, x_b, ctx_b, shared):
    cm = np.zeros((128, 8, 2), np.float32)
    cm[:, :, 0] = inputs['c'][b].reshape(8, 128).T
    cm[:, :, 1] = inputs['c_ctx'].reshape(8, 128).T
    d = dict(shared)
    d.update(x=np.ascontiguousarray(x_b), ctx=np.ascontiguousarray(ctx_b), cmat=cm.reshape(128, 16))
    return d


def shared_inputs(inputs, layers):
    colc, rowc = host_consts(inputs, layers)
    ls = list(layers)
    return dict(
        w_ada=np.ascontiguousarray(inputs['w_ada'][ls]), w_in=np.ascontiguousarray(inputs['w_in'][ls]),
        w_out=np.ascontiguousarray(inputs['w_out'][ls]), colc=colc, rowc=rowc,
        rope=rope_tables().reshape(128, NLT * 96), ident=np.eye(128, dtype=np.float32), hmask=head_masks(),
        biasu=build_bias_tables(np.asarray(inputs['nat_rpb'])[ls]))


_NC_CACHE = {}


def kernel(**inputs):
    inputs = {k: np.asarray(v) for k, v in inputs.items()}
    B = inputs['x'].shape[0]
    layers = list(range(DEPTH))
    key = ('full',)
    if key not in _NC_CACHE:
        bld = make_builder(layers, [l == DEPTH - 1 for l in layers], emit_ctx=False)
        _NC_CACHE[key] = bld.build()
    nc = _NC_CACHE[key]
    shared = shared_inputs(inputs, layers)
    in_maps = [core_inputs(inputs, b, layers, inputs['x'][b], inputs['ctx'][b], shared) for b in range(B)]
    res = run_bass_kernel_spmd(nc, in_maps, core_ids=list(range(B)))
    return np.stack([r["y"] for r in res.results], axis=0).astype(np.float32)
```
